# Optimizing a Trainium2 kernel written in Bass

```python
import jax, jax.numpy as jnp
from jax import lax
import numpy as np

D_MODEL = 1024
BATCH = 16
SEQ = 2048
DEPTH = 1

D_RNN = D_MODEL
RNN_BLOCKS = 16
RNN_BLOCK_W = D_RNN // RNN_BLOCKS
CONV_W = 4
LRU_C = 8.0
HEAD_DIM = 64
N_Q_HEADS = D_MODEL // HEAD_DIM
N_KV_HEADS = 4
GQA_GROUP = N_Q_HEADS // N_KV_HEADS
WINDOW = 128
ATTN_BLOCK = WINDOW
ROPE_THETA = 10000.0
Q_W = N_Q_HEADS * HEAD_DIM
KV_W = N_KV_HEADS * HEAD_DIM
D_FF = 4 * D_MODEL
PLE_DIM = 256
NORM_EPS = 1e-6
IN_WIDTHS = [D_RNN, D_RNN, Q_W, KV_W, KV_W, D_MODEL, D_MODEL]
IN_TOTAL = int(sum(IN_WIDTHS))
SPLIT_IDX = [int(v) for v in np.cumsum(IN_WIDTHS)[:-1]]

kernel_name = 'hybrid_rglru_swa_sink_gated_block'


def _rmsnorm(t, g):
    tf = t.astype(jnp.float32)
    y = tf * lax.rsqrt(jnp.mean(tf * tf, axis=-1, keepdims=True) + NORM_EPS)
    return (y * g.astype(jnp.float32)).astype(t.dtype)


def _rope_tables(S):
    inv = ROPE_THETA ** (-jnp.arange(0, HEAD_DIM, 2, dtype=jnp.float32) / HEAD_DIM)
    ang = jnp.arange(S, dtype=jnp.float32)[:, None] * inv[None, :]
    return jnp.cos(ang), jnp.sin(ang)


def _rope(t, cos, sin):
    tf = t.astype(jnp.float32)
    t1, t2 = jnp.split(tf, 2, axis=-1)
    c = cos[None, :, None, :]
    s = sin[None, :, None, :]
    return jnp.concatenate([t1 * c - t2 * s, t2 * c + t1 * s], axis=-1).astype(t.dtype)


def _causal_conv(t, w, b):
    S = t.shape[1]
    tp = jnp.pad(t, ((0, 0), (CONV_W - 1, 0), (0, 0)))
    out = b + tp[:, 0:S] * w[0]
    for j in range(1, CONV_W):
        out = out + tp[:, j:j + S] * w[j]
    return out


def _rg_lru(xc, w_rg, b_rg, w_ig, b_ig, lam):
    B, S, _ = xc.shape
    xb = xc.reshape(B, S, RNN_BLOCKS, RNN_BLOCK_W)
    r = jax.nn.sigmoid(jnp.einsum('bshi,hij->bshj', xb, w_rg).reshape(B, S, D_RNN) + b_rg)
    i = jax.nn.sigmoid(jnp.einsum('bshi,hij->bshj', xb, w_ig).reshape(B, S, D_RNN) + b_ig)
    log_a = -LRU_C * r.astype(jnp.float32) * jax.nn.softplus(-lam.astype(jnp.float32))
    a = jnp.exp(log_a)
    mult = jnp.sqrt(-jnp.expm1(2.0 * log_a))
    bterm = mult * (i * xc).astype(jnp.float32)

    def combine(left, right):
        a1, b1 = left
        a2, b2 = right
        return a1 * a2, a2 * b1 + b2

    _, h = lax.associative_scan(combine, (a, bterm), axis=1)
    return h.astype(xc.dtype)


def _sliding_window_attention(q, k, v, q_gain, k_gain, sinks, cos, sin):
    B, S, _ = q.shape
    NB = S // ATTN_BLOCK
    q = _rope(_rmsnorm(q.reshape(B, S, N_Q_HEADS, HEAD_DIM), q_gain), cos, sin)
    k = _rope(_rmsnorm(k.reshape(B, S, N_KV_HEADS, HEAD_DIM), k_gain), cos, sin)
    v = v.reshape(B, S, N_KV_HEADS, HEAD_DIM)
    qb = q.reshape(B, NB, ATTN_BLOCK, N_KV_HEADS, GQA_GROUP, HEAD_DIM)

    def band(t):
        tb = t.reshape(B, NB, ATTN_BLOCK, N_KV_HEADS, HEAD_DIM)
        prev = jnp.pad(tb[:, :-1], ((0, 0), (1, 0), (0, 0), (0, 0), (0, 0)))
        return jnp.concatenate([prev, tb], axis=2)

    kb = band(k)
    vb = band(v)
    s = jnp.einsum('bnqkgd,bnckd->bnkgqc', qb, kb).astype(jnp.float32) * (HEAD_DIM ** -0.5)
    qi = jnp.arange(ATTN_BLOCK)[:, None]
    ci = jnp.arange(2 * ATTN_BLOCK)[None, :]
    diff = ATTN_BLOCK + qi - ci
    blk = jnp.arange(NB)[:, None, None]
    valid = (diff >= 0) & (diff < WINDOW) & ((blk - 1) * ATTN_BLOCK + ci >= 0)
    s = jnp.where(valid[None, :, None, None, :, :], s, -jnp.inf)
    sink = sinks.astype(jnp.float32).reshape(N_KV_HEADS, GQA_GROUP)[None, None, :, :, None, None]
    m = jnp.maximum(jnp.max(s, axis=-1, keepdims=True), sink)
    e = jnp.exp(s - m)
    probs = e / (jnp.sum(e, axis=-1, keepdims=True) + jnp.exp(sink - m))
    o = jnp.einsum('bnkgqc,bnckd->bnqkgd', probs.astype(v.dtype), vb)
    return o.reshape(B, S, Q_W)


def setup_inputs(seed: int = 0) -> dict:
    key = jax.random.key(seed)
    ks = jax.random.split(key, 24)
    f32 = jnp.float32
    L = DEPTH

    def nrm(k, shape, scale):
        return jax.random.normal(k, shape, f32) * scale

    u = jax.random.uniform(ks[10], (L, D_RNN), f32, minval=0.9, maxval=0.999)
    s_a = u ** (1.0 / LRU_C)
    lru_lambda = jnp.log(s_a) - jnp.log1p(-s_a)
    return {
        'x': nrm(ks[0], (BATCH, SEQ, D_MODEL), 1.0),
        'p': nrm(ks[1], (DEPTH, BATCH, SEQ, PLE_DIM), 1.0),
        'g_mix': 1.0 + nrm(ks[2], (L, D_MODEL), 0.02),
        'w_in': nrm(ks[3], (L, D_MODEL, IN_TOTAL), D_MODEL ** -0.5),
        'conv_w': nrm(ks[4], (L, CONV_W, D_RNN), CONV_W ** -0.5),
        'conv_b': nrm(ks[5], (L, D_RNN), 0.01),
        'w_rg': nrm(ks[6], (L, RNN_BLOCKS, RNN_BLOCK_W, RNN_BLOCK_W), RNN_BLOCK_W ** -0.5),
        'b_rg': nrm(ks[7], (L, D_RNN), 0.01),
        'w_ig': nrm(ks[8], (L, RNN_BLOCKS, RNN_BLOCK_W, RNN_BLOCK_W), RNN_BLOCK_W ** -0.5),
        'b_ig': nrm(ks[9], (L, D_RNN), 0.01),
        'lru_lambda': lru_lambda,
        'w_rnn_proj': nrm(ks[11], (L, D_RNN, D_MODEL), D_RNN ** -0.5),
        'q_gain': 1.0 + nrm(ks[12], (L, HEAD_DIM), 0.02),
        'k_gain': 1.0 + nrm(ks[13], (L, HEAD_DIM), 0.02),
        'sinks': nrm(ks[14], (L, N_Q_HEADS), 0.5),
        'w_attn_proj': nrm(ks[15], (L, Q_W, D_MODEL), Q_W ** -0.5),
        'w_out': nrm(ks[16], (L, D_MODEL, D_MODEL), D_MODEL ** -0.5),
        'g_mlp': 1.0 + nrm(ks[17], (L, D_MODEL), 0.02),
        'w_up': nrm(ks[18], (L, D_MODEL, D_FF), D_MODEL ** -0.5),
        'w_down': nrm(ks[19], (L, D_FF, D_MODEL), D_FF ** -0.5),
        'g_ple': 1.0 + nrm(ks[20], (L, D_MODEL), 0.02),
        'w_ple_gate': nrm(ks[21], (L, D_MODEL, D_MODEL), D_MODEL ** -0.5),
        'w_ple_proj': nrm(ks[22], (L, PLE_DIM, D_MODEL), PLE_DIM ** -0.5),
    }


def reference(x, p, g_mix, w_in, conv_w, conv_b, w_rg, b_rg, w_ig, b_ig, lru_lambda,
              w_rnn_proj, q_gain, k_gain, sinks, w_attn_proj, w_out, g_mlp, w_up, w_down,
              g_ple, w_ple_gate, w_ple_proj):
    S = x.shape[1]
    cos, sin = _rope_tables(S)
    for l in range(DEPTH):
        h = _rmsnorm(x, g_mix[l])
        z = h @ w_in[l]
        x_rnn, g_rnn, q, k, v, gate_a, gate_b = jnp.split(z, SPLIT_IDX, axis=-1)
        xc = _causal_conv(x_rnn, conv_w[l], conv_b[l])
        hr = _rg_lru(xc, w_rg[l], b_rg[l], w_ig[l], b_ig[l], lru_lambda[l])
        y_a = (hr * jax.nn.gelu(g_rnn)) @ w_rnn_proj[l]
        y_b = _sliding_window_attention(q, k, v, q_gain[l], k_gain[l], sinks[l], cos, sin) @ w_attn_proj[l]
        merged = jax.nn.sigmoid(gate_a) * y_a + jax.nn.sigmoid(gate_b) * y_b
        x = x + merged @ w_out[l]
        hm = _rmsnorm(x, g_mlp[l])
        x = x + jnp.square(jax.nn.relu(hm @ w_up[l])) @ w_down[l]
        e = p[l] @ w_ple_proj[l]
        x = x + e * jax.nn.sigmoid(_rmsnorm(x, g_ple[l]) @ w_ple_gate[l])
    return x
```

```python
import numpy as np
import ml_dtypes
import concourse.bass as bass
import concourse.mybir as mybir
from concourse.bass_utils import run_bass_kernel_spmd
from concourse.alu_op_type import AluOpType as ALU

AF = mybir.ActivationFunctionType
AX = mybir.AxisListType
F32 = mybir.dt.float32
BF16 = mybir.dt.bfloat16

N_CORES = 8
IN_COLS = [0, 2048, 2560, 3072, 1024, 1536, 3584, 4096, 512, 4608, 5120]
NSTREAM = 35


class Sched:
    ENG = ('pe', 'act', 'dve', 'pool', 'sp')

    def __init__(self, nc):
        self.nc = nc
        self.q = {e: [] for e in self.ENG}
        self.sem = {}
        self.cnt = {}
        self.waited = {e: {} for e in self.ENG}
        self.lw = {}
        self.rd = {}
        self.nsem = 0
        for e in ('pe', 'act', 'dve', 'pool'):
            self._mksem(e)

    def _mksem(self, k):
        if k not in self.sem:
            self.sem[k] = self.nc.alloc_semaphore(name="sm%d" % self.nsem)
            self.nsem += 1
            self.cnt[k] = 0

    def _deps(self, e, reads, writes):
        d = {}

        def need(k, v):
            if d.get(k, 0) < v:
                d[k] = v
        for r in reads:
            w = self.lw.get(r)
            if w:
                need(*w)
            if r[0] == 'ps':
                for k, v in self.rd.get(r, {}).items():
                    if k != e:
                        need(k, v)
        for w_ in writes:
            w = self.lw.get(w_)
            if w:
                need(*w)
            for k, v in self.rd.get(w_, {}).items():
                need(k, v)
        waits = []
        for k, v in d.items():
            if self.waited[e].get(k, 0) < v:
                self.waited[e][k] = v
                waits.append((k, v))
        return waits

    def _record(self, semkey, val, reads, writes):
        for r in reads:
            m = self.rd.setdefault(r, {})
            if m.get(semkey, 0) < val:
                m[semkey] = val
        for w_ in writes:
            self.lw[w_] = (semkey, val)
            self.rd[w_] = {}

    def op(self, e, fn, reads=(), writes=()):
        waits = self._deps(e, reads, writes)
        self.cnt[e] += 1
        self.q[e].append((fn, waits, e, 1))
        self._record(e, self.cnt[e], reads, writes)

    def dma(self, e, fn, semkey, reads=(), writes=(), record=True):
        self._mksem(semkey)
        waits = self._deps(e, reads, writes)
        self.cnt[semkey] += 16
        self.q[e].append((fn, waits, semkey, 16))
        if record:
            self._record(semkey, self.cnt[semkey], reads, writes)

    def emit(self, final_keys):
        nc = self.nc
        with nc.Block() as block:
            def mk(e):
                def body(eng):
                    for fn, waits, sk, n in self.q[e]:
                        for k, v in waits:
                            eng.wait_ge(self.sem[k], v)
                        ins = fn(eng)
                        ins.then_inc(self.sem[sk], n)
                    if e == 'sp':
                        for k in final_keys:
                            eng.wait_ge(self.sem[k], self.cnt[k])
                return body
            block.tensor(mk('pe'))
            block.scalar(mk('act'))
            block.vector(mk('dve'))
            block.gpsimd(mk('pool'))
            block.sync(mk('sp'))


def build_nc(n_seq, tps, NSLOT=4, NSCR=8, CONV_AHEAD=6):
    NT = n_seq * tps
    ntok = NT * 512
    nc = bass.Bass("TRN2", target_bir_lowering=False)
    S = Sched(nc)

    def din(name, shape, dt=F32):
        return nc.dram_tensor(name, shape, dt, kind="ExternalInput").ap()
    x_d = din("x", [ntok, 1024])
    p_d = din("p", [ntok, 256])
    w_in_d = din("w_in", [1024, 5632])
    w_rnn_d = din("w_rnn_proj", [1024, 1024])
    w_attn_d = din("w_attn_proj", [1024, 1024])
    w_out_d = din("w_out", [1024, 1024])
    w_up_d = din("w_up", [1024, 4096])
    w_down_d = din("w_down", [4096, 1024])
    w_pg_d = din("w_ple_gate", [1024, 1024])
    w_pp_d = din("w_ple_proj", [256, 1024])
    w_rg_d = din("w_rg", [16, 64, 64])
    w_ig_d = din("w_ig", [16, 64, 64])
    fmv_d = din("fmv", [128, 11, 8])
    bcv_d = din("bcv", [128, 272])
    rope_d = din("rope", [128, 16, 2, 64])
    ident_d = din("ident", [128, 128], BF16)
    mask_d = din("mask", [128, 2, 128], BF16)
    out_d = nc.dram_tensor("out", [ntok, 1024], F32, kind="ExternalOutput").ap()

    s_all = nc.dram_tensor("s_all", [NSTREAM, 1024, 512], BF16, kind="Internal").ap()

    def sb(name, shape, dt):
        return nc.alloc_sbuf_tensor("sb_" + name, shape, dt).ap()
    xt = sb("xt", [128, 4, 1024], F32)
    htm = sb("htm", [128, 4, 1024], BF16)
    mg = htm.rearrange("p s (a b) -> p (s a) b", b=512)
    hT = sb("hT", [128, 8, 512], BF16)
    big = sb("big", [128, 32, 512], BF16)
    qtm = big[:, 24:32, :].rearrange("p (s a) b -> p s (a b)", a=2)
    xcbf = sb("xcbf", [128, 4, 512], BF16)
    yain = sb("yain", [128, 8, 512], BF16)
    e_sb = yain.rearrange("p (s a) b -> p s (a b)", a=2)
    ktm = sb("ktm", [128, 4, 256], BF16)
    QT = sb("QT", [128, 2, 4, 512], BF16)
    KT = sb("KT", [128, 4, 4, 128], BF16)
    VA = sb("VA", [128, 8, 4, 65], BF16)
    xr = sb("xr", [128, 4, 516], BF16)
    dg = sb("dg", [128, 8, 4, 128], BF16)
    aa = sb("aa", [128, 4, 512], F32)
    a2 = sb("a2", [128, 4, 512], F32)
    hist = sb("hist", [128, 8, 4], BF16)
    state = sb("state", [128, 8], F32)
    scr = sb("scr", [128, NSCR, 512], F32)
    NPTB = 8
    NSQB = 4
    sqb = sb("sqb", [128, NSQB, 512], BF16)
    ptb = sb("ptb", [128, NPTB, 512], BF16)
    pbf = sb("pbf", [128, 4, 256], BF16)
    pT = sb("pT", [128, 2, 512], BF16)
    tabraw = sb("tabraw", [128, 4, 2, 64], F32)
    tabt = sb("tabt", [128, 4, 4, 64], F32)
    wsl = sb("wsl", [128, NSLOT, 8, 512], BF16)
    bd = sb("bd", [128, 2, 8, 128], BF16)
    wple = sb("wple", [128, 2, 1024], BF16)
    fmv = sb("fmv", [128, 11, 8], F32)
    bcv = sb("bcv", [128, 272], F32)
    ident = sb("ident", [128, 128], BF16)
    mask = sb("mask", [128, 2, 128], BF16)
    cn = sb("cn", [128, 16], F32)
    hb = sb("hb", [128, 2, 8], F32)
    es = sb("es", [128, 16], F32)
    nh = sb("nh", [128, 8], F32)
    tmp8 = sb("tmp8", [128, 8], F32)
    ss = sb("ss", [128, 4], F32)
    rstd = sb("rstd", [128, 4], F32)
    ssq = sb("ssq", [128, 4, 20], F32)
    rq = sb("rq", [128, 4, 20], F32)
    NDEN = 4
    den = sb("den", [128, NDEN, 4], F32)
    ps = nc.alloc_psum_tensor("ps", [128, 8, 512], F32).ap()

    st = {'bank': 0, 'scr': 0, 'ptb': 0, 'den': 0, 'next_load': 0, 'sqb': 0}

    def PS(b):
        return ps[:, b, :]

    def PSB(b):
        return ps[:, b, :].bitcast(BF16)

    def SC(j):
        return scr[:, j, :]

    def SCb(j):
        return scr[:, j, :].bitcast(BF16)


    def bank():
        b = st['bank']
        st['bank'] = (b + 1) % 8
        return b

    def scr_next():
        j = st['scr']
        st['scr'] = (j + 1) % NSCR
        return j

    def ptb_next():
        j = st['ptb']
        st['ptb'] = (j + 1) % NPTB
        return j

    def den_next():
        j = st['den']
        st['den'] = (j + 1) % NDEN
        return j

    def k_tm(name, s):
        return [(name, 2 * s), (name, 2 * s + 1)]

    def k_all(name, n=8):
        return [(name, i) for i in range(n)]

    def conv_src(i):
        if i < 11:
            return w_in_d[:, IN_COLS[i]:IN_COLS[i] + 512]
        if i < 13:
            return w_attn_d[:, (i - 11) * 512:(i - 10) * 512]
        if i < 15:
            return w_rnn_d[:, (i - 13) * 512:(i - 12) * 512]
        if i < 17:
            return w_out_d[:, (i - 15) * 512:(i - 14) * 512]
        if i < 25:
            return w_up_d[:, (i - 17) * 512:(i - 16) * 512]
        if i < 33:
            n, g = divmod(i - 25, 4)
            return w_down_d[g * 1024:(g + 1) * 1024, n * 512:(n + 1) * 512]
        return w_pg_d[:, (i - 33) * 512:(i - 32) * 512]

    st['next_conv'] = 0

    def convert_upto(i):
        while st['next_conv'] <= min(i, NSTREAM - 1):
            k = st['next_conv']
            src = conv_src(k)
            S.dma('pool', lambda e, k=k, src=src: e.dma_start(out=s_all[k], in_=src), semkey=('cv', k), writes=[('wscr', k)])
            st['next_conv'] += 1

    for dst, src, key in [(fmv, fmv_d, 'c_fmv'), (bcv, bcv_d, 'c_bcv'), (ident, ident_d, 'c_ident'), (mask, mask_d, 'c_mask')]:
        S.dma('sp', lambda e, dst=dst, src=src: e.dma_start(out=dst[:], in_=src[:]), semkey=('cst', key), writes=[(key,)])
    S.op('pool', lambda e: e.memset(bd[:], 0.0), writes=[('bd',)])
    for gi, wsrc in enumerate([w_rg_d, w_ig_d]):
        v = wsrc.rearrange("(c two) i j -> two i c j", two=2)
        S.dma('pool', lambda e, gi=gi, v=v: e.dma_start(out=bd[0:64, gi, :, 0:64], in_=v[0]), semkey=('bdl',),
              writes=[('bd',)], record=False)
        S.dma('pool', lambda e, gi=gi, v=v: e.dma_start(out=bd[64:128, gi, :, 64:128], in_=v[1]), semkey=('bdl',),
              writes=[('bd',)], record=False)
    S.lw[('bd',)] = (('bdl',), S.cnt[('bdl',)])
    S.dma('pool', lambda e: e.dma_start(out=wple[:], in_=w_pp_d.rearrange("(kc p) n -> p kc n", p=128)), semkey=('wpl',),
          writes=[('wple',)])
    S.op('pool', lambda e: e.memset(nh[:], -0.5), writes=[('nh',)])
    S.op('pool', lambda e: e.memset(VA[:], 1.0), writes=k_all('VA'))
    S.op('pool', lambda e: e.memset(QT[:], 0.0), writes=[('QT', a_, g_) for a_ in range(2) for g_ in range(4)])
    S.op('pool', lambda e: e.memset(KT[:], 0.0), writes=[('KT', i_) for i_ in range(4)])
    S.op('act', lambda e: e.activation(out=tmp8[:], in_=fmv[:, 10, :], func=AF.Exp, scale=-1.0), reads=[('c_fmv',)], writes=[('tmp8',)])
    S.op('act', lambda e: e.activation(out=tmp8[:], in_=tmp8[:], func=AF.Ln, bias=1.0), reads=[('tmp8',)], writes=[('tmp8',)])
    S.op('dve', lambda e: e.tensor_scalar(out=cn[:, 0:8], in0=tmp8[:], scalar1=-8.0, scalar2=None, op0=ALU.mult), reads=[('tmp8',)], writes=[('cn',)])
    S.op('dve', lambda e: e.tensor_scalar(out=cn[:, 8:16], in0=tmp8[:], scalar1=-4.0, scalar2=None, op0=ALU.mult), reads=[('tmp8',)], writes=[('cn',)])
    S.op('dve', lambda e: e.tensor_scalar(out=hb[:], in0=fmv[:, 8:10, :], scalar1=0.5, scalar2=None, op0=ALU.mult), reads=[('c_fmv',)], writes=[('hb',)])
    S.op('act', lambda e: e.activation(out=es[:], in_=bcv[:, 256:272], func=AF.Exp), reads=[('c_bcv',)], writes=[('es',)])

    for c in range(8):
        for tap in range(4):
            S.op('dve', lambda e, c=c, tap=tap: e.tensor_scalar(out=dg[:, c, tap, :], in0=ident[:], scalar1=fmv[:, 3 + tap, c:c + 1], scalar2=None, op0=ALU.mult),
                 reads=[('c_ident',), ('c_fmv',)], writes=[('dg',)])

    convert_upto(3)

    total_loads = NT * NSTREAM

    def slot_src(i):
        return s_all[i].rearrange("(kc p) n -> p kc n", p=128), i

    def prefetch(upto):
        while st['next_load'] <= min(upto, total_loads - 1):
            k = st['next_load']
            pos = k % NSLOT
            convert_upto(k + CONV_AHEAD)
            src, mname = slot_src(k % NSTREAM)
            S.dma('sp', lambda e, pos=pos, src=src: e.dma_start(out=wsl[:, pos], in_=src), semkey=('wl', pos),
                  reads=[('wscr', mname)], writes=[('wsl', pos)])
            st['next_load'] += 1

    st['released'] = -1

    def slot_acquire(k):
        assert k <= st['released'] + NSLOT, (k, st['released'])
        prefetch(k)
        return k % NSLOT

    def slot_release(k):
        assert k == st['released'] + 1, (k, st['released'])
        st['released'] = k
        prefetch(k + NSLOT)

    def pe_fine(mk, pos, b, rkeys):
        for kc in range(8):
            S.op('pe', mk(kc), reads=[('wsl', pos), (rkeys, kc)], writes=([('ps', b)] if kc in (0, 7) else []))

    def mm_fm(pos, oc, rhs3, b):
        def f(e):
            for kc in range(8):
                ins = e.matmul(PS(b), lhsT=wsl[:, pos, kc, oc * 128:(oc + 1) * 128], rhs=rhs3[:, kc, :],
                               start=(kc == 0), stop=(kc == 7))
            return ins
        return f

    def mm_tm(pos, s, lhs3, b):
        def f(e):
            for kc in range(8):
                ins = e.matmul(PS(b), lhsT=lhs3[:, kc, s * 128:(s + 1) * 128], rhs=wsl[:, pos, kc, :],
                               start=(kc == 0), stop=(kc == 7))
            return ins
        return f

    xstage = big[:, 0:16, :].bitcast(F32).rearrange("p (s a) b -> p s (a b)", a=4)

    def k_xs(s):
        return [('big', 4 * s + i) for i in range(4)]

    def x_load(t, s):
        r0 = t * 512 + s * 128
        S.dma('sp', lambda e, r0=r0, s=s: e.dma_start(out=xstage[:, s, :], in_=x_d[r0:r0 + 128, :]), semkey=('xl', s),
              writes=k_xs(s))

    def x_commit():
        for s in range(4):
            S.dma('sp', lambda e, s=s: e.dma_start(out=xt[:, s, :], in_=xstage[:, s, :]), semkey=('xc', s), reads=k_xs(s), writes=[('xt', s)])

    def norm_a(gi, staged=False):
        for s in range(4):
            j_ = scr_next()
            src = xstage[:, s, :] if staged else xt[:, s, :]
            skeys = k_xs(s) if staged else [('xt', s)]
            S.op('act', lambda e, s=s, j_=j_, src=src: e.activation(out=SCb(j_), in_=src, func=AF.Square, accum_out=ss[:, s:s + 1]),
                 reads=skeys, writes=[('scr', j_), ('ss', s)])
            S.op('pool', lambda e, s=s: e.tensor_scalar(out=rstd[:, s:s + 1], in0=ss[:, s:s + 1], scalar1=1.0 / 1024, scalar2=1e-6,
                                                        op0=ALU.mult, op1=ALU.add), reads=[('ss', s)], writes=[('rstd', s)])
            S.op('pool', lambda e, s=s: e.tensor_tensor(out=rstd[:, s:s + 1], in0=rstd[:, s:s + 1], in1=nh[:, 0:1], op=ALU.pow),
                 reads=[('rstd', s), ('nh',)], writes=[('rstd', s)])
            S.op('dve', lambda e, s=s, src=src: e.tensor_scalar(out=htm[:, s, :], in0=src, scalar1=rstd[:, s:s + 1], scalar2=None, op0=ALU.mult),
                 reads=skeys + [('rstd', s)], writes=k_tm('htm', s))

    def norm_stage(gi):
        norm_a(gi)
        norm_b(gi)

    def norm_b(gi):
        for c in range(8):
            b = bank()

            def f(e, c=c, b=b):
                for s in range(4):
                    ins = e.transpose(out=PSB(b)[:, s * 128:(s + 1) * 128], in_=htm[:, s, c * 128:(c + 1) * 128], identity=ident[:])
                return ins
            S.op('pe', f, reads=k_all('htm') + [('c_ident',)], writes=[('ps', b)])
            if c % 2 == 0:
                S.op('act', lambda e, c=c, b=b: e.activation(out=hT[:, c, :], in_=PSB(b)[:, 0:512], func=AF.Copy, scale=fmv[:, gi, c:c + 1]),
                     reads=[('ps', b), ('c_fmv',)], writes=[('hT', c)])
            else:
                S.op('dve', lambda e, c=c, b=b: e.tensor_scalar(out=hT[:, c, :], in0=PSB(b)[:, 0:512], scalar1=fmv[:, gi, c:c + 1], scalar2=None, op0=ALU.mult),
                     reads=[('ps', b), ('c_fmv',)], writes=[('hT', c)])

    def rope_chain(s, b, col0, nh_, ti, dst3, hoff, dkeys):
        w = nh_ * 64
        isq = st['sqb']
        st['sqb'] = (isq + 1) % NSQB
        S.op('act', lambda e: e.activation(out=sqb[:, isq, 0:w], in_=PS(b)[:, col0:col0 + w], func=AF.Square), reads=[('ps', b)], writes=[('sqb', isq)])
        S.op('dve', lambda e: e.tensor_reduce(out=ssq[:, s, hoff:hoff + nh_], in_=sqb[:, isq, 0:w].rearrange("p (h d) -> p h d", d=64), axis=AX.X, op=ALU.add),
             reads=[('sqb', isq)], writes=[('ssq', s, hoff)])
        S.op('pool', lambda e: e.tensor_scalar(out=rq[:, s, hoff:hoff + nh_], in0=ssq[:, s, hoff:hoff + nh_], scalar1=1.0 / 64, scalar2=1e-6, op0=ALU.mult, op1=ALU.add),
             reads=[('ssq', s, hoff)], writes=[('rq', s, hoff)])
        S.op('pool', lambda e: e.tensor_tensor(out=rq[:, s, hoff:hoff + nh_], in0=rq[:, s, hoff:hoff + nh_], in1=nh[:, 0:nh_], op=ALU.pow),
             reads=[('rq', s, hoff), ('nh',)], writes=[('rq', s, hoff)])
        j1 = scr_next()
        j2 = scr_next()
        q3 = PS(b)[:, col0:col0 + w].rearrange("p (h d) -> p h d", d=64)
        t1 = SC(j1)[:, 0:w].rearrange("p (h d) -> p h d", d=64)
        t2 = SC(j2)[:, 0:w].rearrange("p (h d) -> p h d", d=64)
        cc = tabt[:, s, ti, :].unsqueeze(1).broadcast_to([128, nh_, 64])
        wlo = tabt[:, s, ti + 1, 0:32].unsqueeze(1).broadcast_to([128, nh_, 32])
        whi = tabt[:, s, ti + 1, 32:64].unsqueeze(1).broadcast_to([128, nh_, 32])
        S.op('dve', lambda e: e.tensor_tensor(out=t1, in0=q3, in1=cc, op=ALU.mult), reads=[('ps', b), ('tabt',)], writes=[('scr', j1)])
        S.op('dve', lambda e: e.tensor_tensor(out=t2[:, :, 0:32], in0=q3[:, :, 32:64], in1=wlo, op=ALU.mult), reads=[('ps', b), ('tabt',)], writes=[('scr', j2)])
        S.op('dve', lambda e: e.tensor_tensor(out=t2[:, :, 32:64], in0=q3[:, :, 0:32], in1=whi, op=ALU.mult), reads=[('ps', b), ('tabt',)], writes=[('scr', j2)])
        S.op('pool', lambda e: e.tensor_tensor(out=SC(j1)[:, 0:w], in0=SC(j1)[:, 0:w], in1=SC(j2)[:, 0:w], op=ALU.add), reads=[('scr', j1), ('scr', j2)], writes=[('scr', j1)])
        rb = rq[:, s, hoff:hoff + nh_].unsqueeze(2).broadcast_to([128, nh_, 64])

        def tail():
            S.op('dve', lambda e: e.tensor_tensor(out=dst3, in0=t1, in1=rb, op=ALU.mult), reads=[('scr', j1), ('rq', s, hoff)], writes=dkeys)
        rope_flush()
        rope_pending.append(tail)

    rope_pending = []

    def rope_flush():
        while rope_pending:
            rope_pending.pop(0)()

    gel = big[:, 0:8, :]
    ta = big[:, 8:16, :]
    tb = big[:, 16:24, :]

    def rnn_chunk_front(c):
        cc_ = c % 4
        br = bank()
        S.op('pe', lambda e: e.matmul(PS(br), lhsT=bd[:, 0, c, :], rhs=xcbf[:, cc_, :], start=True, stop=True), reads=[('bd',), ('xcbf', cc_)], writes=[('ps', br)])
        bi = bank()
        S.op('pe', lambda e: e.matmul(PS(bi), lhsT=bd[:, 1, c, :], rhs=xcbf[:, cc_, :], start=True, stop=True), reads=[('bd',), ('xcbf', cc_)], writes=[('ps', bi)])
        jr = scr_next()
        S.op('act', lambda e: e.activation(out=SC(jr), in_=PS(br), func=AF.Tanh, scale=0.5, bias=hb[:, 0, c:c + 1]), reads=[('ps', br), ('hb',)], writes=[('scr', jr)])
        S.op('act', lambda e: e.activation(out=aa[:, cc_, :], in_=SC(jr), func=AF.Exp, scale=cn[:, 8 + c:9 + c], bias=cn[:, 8 + c:9 + c]), reads=[('scr', jr), ('cn',)], writes=[('aa', cc_)])
        S.op('act', lambda e: e.activation(out=a2[:, cc_, :], in_=SC(jr), func=AF.Exp, scale=cn[:, c:c + 1], bias=cn[:, c:c + 1]), reads=[('scr', jr), ('cn',)], writes=[('a2', cc_)])
        ji = scr_next()
        S.op('act', lambda e: e.activation(out=SC(ji), in_=PS(bi), func=AF.Tanh, scale=0.5, bias=hb[:, 1, c:c + 1]), reads=[('ps', bi), ('hb',)], writes=[('scr', ji)])
        S.op('dve', lambda e: e.scalar_tensor_tensor(out=xcbf[:, cc_, :], in0=SC(ji), scalar=1.0, in1=xcbf[:, cc_, :], op0=ALU.add, op1=ALU.mult),
             reads=[('scr', ji), ('xcbf', cc_)], writes=[('xcbf', cc_)])
        S.op('dve', lambda e: e.tensor_scalar(out=a2[:, cc_, :], in0=a2[:, cc_, :], scalar1=1.0, scalar2=None, op0=ALU.min), reads=[('a2', cc_)], writes=[('a2', cc_)])

    def rnn_sqrt(c):
        cc_ = c % 4
        S.op('act', lambda e: e.activation(out=a2[:, cc_, :], in_=a2[:, cc_, :], func=AF.Sqrt, scale=-0.25, bias=0.25), reads=[('a2', cc_)], writes=[('a2', cc_)])

    def rnn_chunk_back(c):
        cc_ = c % 4
        S.op('pool', lambda e: e.tensor_tensor(out=a2[:, cc_, :], in0=a2[:, cc_, :], in1=xcbf[:, cc_, :], op=ALU.mult),
             reads=[('a2', cc_), ('xcbf', cc_)], writes=[('a2', cc_)])
        jh = scr_next()
        S.op('dve', lambda e: e.tensor_tensor_scan(out=SC(jh), data0=aa[:, cc_, :], data1=a2[:, cc_, :], initial=state[:, c:c + 1], op0=ALU.mult, op1=ALU.add),
             reads=[('aa', cc_), ('a2', cc_), ('state', c)], writes=[('scr', jh)])
        S.op('dve', lambda e: e.tensor_copy(out=state[:, c:c + 1], in_=SC(jh)[:, 511:512]), reads=[('scr', jh)], writes=[('state', c)])
        S.op('pool', lambda e: e.tensor_tensor(out=yain[:, c, :], in0=SC(jh), in1=gel[:, c, :], op=ALU.mult), reads=[('scr', jh), ('big', c)], writes=[('yain', c)])

    def rope_tables(jj):
        S.dma('sp', lambda e, jj=jj: e.dma_start(out=tabraw[:], in_=rope_d[:, jj * 4:(jj + 1) * 4]), semkey=('tl',), writes=[('tabraw',)])
        for ti, g0 in [(0, 0), (1, 64), (2, 128), (3, 192)]:
            S.op('dve', lambda e, ti=ti, g0=g0: e.tensor_tensor(out=tabt[:, :, ti, :], in0=tabraw[:, :, ti % 2, :],
                                                               in1=bcv[:, g0:g0 + 64].unsqueeze(1).broadcast_to([128, 4, 64]), op=ALU.mult),
                 reads=[('tabraw',), ('c_bcv',)], writes=[('tabt',)])

    for s in range(4):
        x_load(0, s)
    rope_tables(0)
    norm_a(0, staged=True)
    prefetch(NSLOT - 1)

    for t in range(NT):
        j = t % tps
        base = t * NSTREAM
        if j == 0:
            S.op('pool', lambda e: e.memset(hist[:], 0.0), writes=k_all('hist'))
            S.op('pool', lambda e: e.memset(state[:], 0.0), writes=k_all('state'))
        r0 = t * 512
        S.dma('pool', lambda e, r0=r0: e.dma_start(out=pbf[:], in_=p_d[r0:r0 + 512, :].rearrange("(s p) k -> p s k", p=128)), semkey=('pl',),
              writes=[('pbf',)])

        x_commit()
        norm_b(0)

        def q_block(n):
            pos = slot_acquire(base + 1 + n)
            for s in range(4):
                b = bank()
                S.op('pe', mm_tm(pos, s, hT, b), reads=[('wsl', pos)] + k_all('hT'), writes=[('ps', b)])
                dst3 = qtm[:, s, n * 512:(n + 1) * 512].rearrange("p (h d) -> p h d", d=64)
                rope_chain(s, b, 0, 8, 0, dst3, n * 8, [('big', 24 + 2 * s + n)])
            slot_release(base + 1 + n)

        def kv_block():
            pos = slot_acquire(base + 3)
            for s in range(4):
                blk = j * 4 + s
                b = bank()
                S.op('pe', mm_tm(pos, s, hT, b), reads=[('wsl', pos)] + k_all('hT'), writes=[('ps', b)])
                S.op('dve', lambda e, b=b, blk=blk: e.tensor_copy(out=VA[:, blk % 8, :, 0:64], in_=PS(b)[:, 256:512].rearrange("p (h d) -> p h d", d=64)),
                     reads=[('ps', b)], writes=[('VA', blk % 8)])
                dst3 = ktm[:, s, :].rearrange("p (h d) -> p h d", d=64)
                rope_chain(s, b, 0, 4, 2, dst3, 16, [('ktm', s)])
            slot_release(base + 3)

        def xrnn_a(n, fine=False):
            kk = base + (0 if n == 0 else 8)
            pos = slot_acquire(kk)
            for oc in range(4):
                c = n * 4 + oc
                b = bank()
                xb = c % 4
                if fine and oc == 0:
                    pe_fine(lambda kc, pos=pos, b=b: (lambda e: e.matmul(PS(b), lhsT=wsl[:, pos, kc, 0:128], rhs=hT[:, kc, :], start=(kc == 0), stop=(kc == 7))), pos, b, 'hT')
                else:
                    S.op('pe', mm_fm(pos, oc, hT, b), reads=[('wsl', pos)] + k_all('hT'), writes=[('ps', b)])
                S.op('act', lambda e, b=b, xb=xb: e.activation(out=xr[:, xb, 3:515], in_=PS(b), func=AF.Copy), reads=[('ps', b)], writes=[('xr', xb)])
                S.op('dve', lambda e, c=c, xb=xb: e.tensor_copy(out=xr[:, xb, 0:3], in_=hist[:, c, 0:3]), reads=[('hist', c)], writes=[('xr', xb)])
            slot_release(kk)

        def xrnn_b(n):
            for oc in range(4):
                c = n * 4 + oc
                xb = c % 4
                bc = bank()

                def fc(e, c=c, xb=xb, bc=bc):
                    for tap in range(4):
                        ins = e.matmul(PS(bc), lhsT=dg[:, c, tap, :], rhs=xr[:, xb, tap:tap + 512], start=(tap == 0), stop=(tap == 3))
                    return ins
                S.op('pe', fc, reads=[('xr', xb), ('dg',)], writes=[('ps', bc)])
                S.op('act', lambda e, c=c, bc=bc: e.activation(out=xcbf[:, c % 4, :], in_=PS(bc), func=AF.Identity, bias=fmv[:, 7, c:c + 1]),
                     reads=[('ps', bc), ('c_fmv',)], writes=[('xcbf', c % 4)])
                S.op('dve', lambda e, c=c, xb=xb: e.tensor_copy(out=hist[:, c, 0:3], in_=xr[:, xb, 512:515]), reads=[('xr', xb)], writes=[('hist', c)])

        def gelu_block(n):
            pos = slot_acquire(base + 4 + n)
            for oc in range(4):
                c = n * 4 + oc
                b = bank()
                S.op('pe', mm_fm(pos, oc, hT, b), reads=[('wsl', pos)] + k_all('hT'), writes=[('ps', b)])
                S.op('act', lambda e, b=b, c=c: e.activation(out=gel[:, c, :], in_=PS(b), func=AF.Gelu_apprx_tanh), reads=[('ps', b)], writes=[('big', c)])
            slot_release(base + 4 + n)

        def gate_block(n):
            kk = base + (6 + n if n < 2 else 7 + n)
            pos = slot_acquire(kk)
            for oc in range(4):
                cg = n * 4 + oc
                b = bank()
                S.op('pe', mm_fm(pos, oc, hT, b), reads=[('wsl', pos)] + k_all('hT'), writes=[('ps', b)])
                S.op('act', lambda e, b=b, cg=cg: e.activation(out=big[:, 8 + cg, :], in_=PS(b), func=AF.Tanh, scale=0.5), reads=[('ps', b)], writes=[('big', 8 + cg)])
            slot_release(kk)

        def attn_T(s):
            blk = j * 4 + s
            b = bank()

            def fk(e, b=b, s=s):
                for g in range(4):
                    ins = e.transpose(out=PSB(b)[0:64, g * 128:(g + 1) * 128], in_=ktm[:, s, g * 64:(g + 1) * 64], identity=ident[:])
                return ins
            S.op('pe', fk, reads=[('ktm', s), ('c_ident',)], writes=[('ps', b)])
            S.op('dve', lambda e, b=b, blk=blk: e.tensor_copy(out=KT[0:64, blk % 4, :, :].rearrange("p g k -> p (g k)"), in_=PSB(b)[0:64, 0:512]),
                 reads=[('ps', b)], writes=[('KT', blk % 4)])
            for g in range(4):
                b = bank()

                def fq(e, b=b, s=s, g=g):
                    for i in range(4):
                        h = 4 * g + i
                        ins = e.transpose(out=PSB(b)[0:64, i * 128:(i + 1) * 128], in_=qtm[:, s, h * 64:(h + 1) * 64], identity=ident[:])
                    return ins
                S.op('pe', fq, reads=[('big', 24 + 2 * s + (g // 2)), ('c_ident',)], writes=[('ps', b)])
                if g % 2 == 0:
                    S.op('act', lambda e, b=b, s=s, g=g: e.activation(out=QT[0:64, s % 2, g, :], in_=PSB(b)[0:64, 0:512], func=AF.Copy), reads=[('ps', b)], writes=[('QT', s % 2, g)])
                else:
                    S.op('dve', lambda e, b=b, s=s, g=g: e.tensor_copy(out=QT[0:64, s % 2, g, :], in_=PSB(b)[0:64, 0:512]), reads=[('ps', b)], writes=[('QT', s % 2, g)])

        pend = []

        def flush_pv():
            bo_g, pts, g, s = pend.pop(0)
            bo = bank()

            def fpv(e, bo=bo, pts=pts, g=g):
                for i in range(4):
                    for n_, (ip, kblk) in enumerate(pts):
                        ins = e.matmul(PS(bo)[:, i * 65:(i + 1) * 65], lhsT=ptb[:, ip, i * 128:(i + 1) * 128], rhs=VA[:, kblk % 8, g, :],
                                       start=(n_ == 0), stop=(n_ == len(pts) - 1))
                return ins
            S.op('pe', fpv, reads=[('ptb', ip) for ip, _ in pts] + [('VA', kblk % 8) for _, kblk in pts], writes=[('ps', bo)])
            dn = den_next()
            o3 = PS(bo)[:, 0:260].rearrange("p (i d) -> p i d", d=65)
            S.op('dve', lambda e, o3=o3, dn=dn, g=g: e.tensor_tensor(out=den[:, dn, :], in0=o3[:, :, 64], in1=es[:, 4 * g:4 * g + 4], op=ALU.add),
                 reads=[('ps', bo), ('es',)], writes=[('den', dn)])
            S.op('dve', lambda e, dn=dn: e.reciprocal(out=den[:, dn, :], in_=den[:, dn, :]), reads=[('den', dn)], writes=[('den', dn)])
            S.op('dve', lambda e, o3=o3, dn=dn, g=g, s=s: e.tensor_tensor(out=htm[:, s, g * 256:(g + 1) * 256].rearrange("p (i d) -> p i d", d=64),
                                                                         in0=o3[:, :, 0:64], in1=den[:, dn, :].unsqueeze(2).broadcast_to([128, 4, 64]), op=ALU.mult),
                 reads=[('ps', bo), ('den', dn)], writes=k_tm('htm', s))

        def attn_gen():
          for s in range(4):
            blk = j * 4 + s
            kbs = ([(blk - 1, 1)] if blk > 0 else []) + [(blk, 0)]
            for g in range(4):
                pts = []
                for kblk, m in kbs:
                    b = bank()
                    S.op('pe', lambda e, b=b, kblk=kblk, g=g, s=s: e.matmul(PS(b), lhsT=KT[:, kblk % 4, g, :], rhs=QT[:, s % 2, g, :], start=True, stop=True),
                         reads=[('KT', kblk % 4), ('QT', s % 2, g)], writes=[('ps', b)])
                    ip = ptb_next()
                    S.op('act', lambda e, b=b, ip=ip: e.activation(out=ptb[:, ip, :], in_=PS(b), func=AF.Exp, scale=0.125), reads=[('ps', b)], writes=[('ptb', ip)])
                    S.op('pool', lambda e, ip=ip, m=m: e.tensor_tensor(out=ptb[:, ip, :].rearrange("p (i q) -> p i q", q=128),
                                                                     in0=ptb[:, ip, :].rearrange("p (i q) -> p i q", q=128),
                                                                     in1=mask[:, m, :].unsqueeze(1).broadcast_to([128, 4, 128]), op=ALU.mult),
                         reads=[('ptb', ip), ('c_mask',)], writes=[('ptb', ip)])
                    pts.append((ip, kblk))
                pend.append((None, pts, g, s))
                if len(pend) > 2:
                    flush_pv()
                if g == 1 and s < 3:
                    attn_T(s + 1)
                yield

        ag = attn_gen()

        def A(n):
            for _ in range(n):
                next(ag, None)

        xrnn_a(0, fine=True)
        q_block(0)
        xrnn_b(0)
        q_block(1)
        kv_block()
        rope_flush()
        if t + 1 < NT:
            rope_tables((t + 1) % tps)
        gelu_block(0)
        gelu_block(1)
        attn_T(0)
        A(2)
        rnn_chunk_front(0)
        rnn_chunk_front(1)
        A(2)
        gate_block(0)
        rnn_chunk_front(2)
        rnn_chunk_front(3)
        A(2)
        gate_block(1)
        for c in range(0, 4):
            rnn_sqrt(c)
        A(2)
        xrnn_a(1)
        rnn_chunk_back(0)
        rnn_chunk_back(1)
        A(2)
        gate_block(2)
        rnn_chunk_back(2)
        rnn_chunk_back(3)
        A(2)
        xrnn_b(1)
        rnn_chunk_front(4)
        rnn_chunk_front(5)
        A(2)
        rnn_chunk_front(6)
        rnn_chunk_front(7)
        gate_block(3)
        A(2)
        A(16)
        while pend:
            flush_pv()
        for c in range(4, 8):
            rnn_sqrt(c)
        for c in range(8):
            b = bank()

            def fo(e, c=c, b=b):
                for s in range(4):
                    ins = e.transpose(out=PSB(b)[:, s * 128:(s + 1) * 128], in_=htm[:, s, c * 128:(c + 1) * 128], identity=ident[:])
                return ins
            S.op('pe', fo, reads=k_all('htm') + [('c_ident',)], writes=[('ps', b)])
            if c % 2 == 0:
                S.op('act', lambda e, c=c, b=b: e.activation(out=hT[:, c, :], in_=PSB(b)[:, 0:512], func=AF.Copy), reads=[('ps', b)], writes=[('hT', c)])
            else:
                S.op('dve', lambda e, c=c, b=b: e.tensor_copy(out=hT[:, c, :], in_=PSB(b)[:, 0:512]), reads=[('ps', b)], writes=[('hT', c)])

        for c in range(4, 8):
            rnn_chunk_back(c)

        for n in range(2):
            pos = slot_acquire(base + 11 + n)
            for oc in range(4):
                c = n * 4 + oc
                b = bank()
                S.op('pe', mm_fm(pos, oc, hT, b), reads=[('wsl', pos)] + k_all('hT'), writes=[('ps', b)])
                S.op('dve', lambda e, b=b, c=c: e.scalar_tensor_tensor(out=mg[:, c, :], in0=tb[:, c, :], scalar=1.0, in1=PS(b), op0=ALU.add, op1=ALU.mult),
                     reads=[('ps', b), ('big', 16 + c)], writes=[('htm', c)])
            slot_release(base + 11 + n)
        for n in range(2):
            pos = slot_acquire(base + 13 + n)
            for oc in range(4):
                c = n * 4 + oc
                b = bank()
                S.op('pe', mm_fm(pos, oc, yain, b), reads=[('wsl', pos)] + k_all('yain'), writes=[('ps', b)])
                jt = scr_next()
                S.op('dve', lambda e, b=b, c=c, jt=jt: e.scalar_tensor_tensor(out=SCb(jt)[:, 0:512], in0=ta[:, c, :], scalar=1.0, in1=PS(b), op0=ALU.add, op1=ALU.mult),
                     reads=[('ps', b), ('big', 8 + c)], writes=[('scr', jt)])
                S.op('pool', lambda e, c=c, jt=jt: e.tensor_tensor(out=mg[:, c, :], in0=mg[:, c, :], in1=SCb(jt)[:, 0:512], op=ALU.add),
                     reads=[('scr', jt), ('htm', c)], writes=[('htm', c)])
            slot_release(base + 13 + n)

        pos0 = slot_acquire(base + 15)
        pos1 = slot_acquire(base + 16)
        for s in range(4):
            for n, pos in enumerate((pos0, pos1)):
                b = bank()
                S.op('pe', mm_tm(pos, s, mg, b), reads=[('wsl', pos)] + k_all('htm'), writes=[('ps', b)])
                S.op('dve', lambda e, b=b, s=s, n=n: e.scalar_tensor_tensor(out=xt[:, s, n * 512:(n + 1) * 512], in0=PS(b), scalar=0.5,
                                                                          in1=xt[:, s, n * 512:(n + 1) * 512], op0=ALU.mult, op1=ALU.add),
                     reads=[('ps', b), ('xt', s)], writes=[('xt', s)])
        slot_release(base + 15)
        slot_release(base + 16)

        norm_a(1)
        for kc in range(2):
            b = bank()

            def fp(e, kc=kc, b=b):
                for s in range(4):
                    ins = e.transpose(out=PSB(b)[:, s * 128:(s + 1) * 128], in_=pbf[:, s, kc * 128:(kc + 1) * 128], identity=ident[:])
                return ins
            S.op('pe', fp, reads=[('pbf',), ('c_ident',)], writes=[('ps', b)])
            S.op('dve', lambda e, kc=kc, b=b: e.tensor_copy(out=pT[:, kc, :], in_=PSB(b)[:, 0:512]), reads=[('ps', b)], writes=[('pT', kc)])
        for s in range(4):
            for n in range(2):
                be = bank()

                def fe(e, be=be, s=s, n=n):
                    for kc in range(2):
                        ins = e.matmul(PS(be), lhsT=pT[:, kc, s * 128:(s + 1) * 128], rhs=wple[:, kc, n * 512:(n + 1) * 512], start=(kc == 0), stop=(kc == 1))
                    return ins
                S.op('pe', fe, reads=[('pT', 0), ('pT', 1), ('wple',)], writes=[('ps', be)])
                S.op('act', lambda e, be=be, s=s, n=n: e.activation(out=e_sb[:, s, n * 512:(n + 1) * 512], in_=PS(be), func=AF.Copy),
                     reads=[('ps', be)], writes=[('yain', 2 * s + n)])
        norm_b(1)
        for n in range(8):
            pos = slot_acquire(base + 17 + n)
            for oc in range(4):
                u = n * 4 + oc
                b = bank()
                if u == 0:
                    pe_fine(lambda kc, pos=pos, b=b: (lambda e: e.matmul(PS(b), lhsT=wsl[:, pos, kc, 0:128], rhs=hT[:, kc, :], start=(kc == 0), stop=(kc == 7))), pos, b, 'hT')
                else:
                    S.op('pe', mm_fm(pos, oc, hT, b), reads=[('wsl', pos)] + k_all('hT'), writes=[('ps', b)])
                jt = scr_next()
                S.op('act', lambda e, b=b, jt=jt: e.activation(out=SCb(jt)[:, 0:512], in_=PS(b), func=AF.Relu), reads=[('ps', b)], writes=[('scr', jt)])
                eng = 'dve' if u % 2 == 0 else 'pool'
                S.op(eng, lambda e, u=u, jt=jt: e.tensor_tensor(out=big[:, u, :], in0=SCb(jt)[:, 0:512], in1=SCb(jt)[:, 0:512], op=ALU.mult),
                     reads=[('scr', jt)], writes=[('big', u)])
            slot_release(base + 17 + n)
        for n in range(2):
            B = [bank() for _ in range(4)]
            for g in range(4):
                pos = slot_acquire(base + 25 + n * 4 + g)
                for s in range(4):
                    def fd(e, pos=pos, g=g, s=s, bb=B[s]):
                        for kc in range(8):
                            ins = e.matmul(PS(bb), lhsT=big[:, g * 8 + kc, s * 128:(s + 1) * 128], rhs=wsl[:, pos, kc, :],
                                           start=(g == 0 and kc == 0), stop=(g == 3 and kc == 7))
                        return ins
                    wr = [('ps', B[s])] if g in (0, 3) else []
                    S.op('pe', fd, reads=[('wsl', pos)] + [('big', g * 8 + kc) for kc in range(8)], writes=wr)
                slot_release(base + 25 + n * 4 + g)
            for s in range(4):
                S.op('dve', lambda e, s=s, n=n, bb=B[s]: e.tensor_tensor(out=xt[:, s, n * 512:(n + 1) * 512], in0=PS(bb), in1=xt[:, s, n * 512:(n + 1) * 512], op=ALU.add),
                     reads=[('ps', B[s]), ('xt', s)], writes=[('xt', s)])

        if t + 1 < NT:
            for s in range(4):
                x_load(t + 1, s)
        norm_stage(2)
        if t + 1 < NT:
            norm_a(0, staged=True)
        pos0 = slot_acquire(base + 33)
        pos1 = slot_acquire(base + 34)
        for s in range(4):
            for n, pos in enumerate((pos0, pos1)):
                bg = bank()
                if s == 0 and n == 0:
                    pe_fine(lambda kc, pos=pos, bg=bg: (lambda e: e.matmul(PS(bg), lhsT=hT[:, kc, 0:128], rhs=wsl[:, pos, kc, :], start=(kc == 0), stop=(kc == 7))), pos, bg, 'hT')
                else:
                    S.op('pe', mm_tm(pos, s, hT, bg), reads=[('wsl', pos)] + k_all('hT'), writes=[('ps', bg)])
                jt = scr_next()
                S.op('act', lambda e, bg=bg, jt=jt: e.activation(out=SC(jt), in_=PS(bg), func=AF.Tanh, scale=0.5), reads=[('ps', bg)], writes=[('scr', jt)])
                S.op('dve', lambda e, jt=jt, s=s, n=n: e.scalar_tensor_tensor(out=SC(jt), in0=SC(jt), scalar=1.0, in1=e_sb[:, s, n * 512:(n + 1) * 512], op0=ALU.add, op1=ALU.mult),
                     reads=[('scr', jt), ('yain', 2 * s + n)], writes=[('scr', jt)])
                S.op('dve', lambda e, jt=jt, s=s, n=n: e.scalar_tensor_tensor(out=xt[:, s, n * 512:(n + 1) * 512], in0=SC(jt), scalar=0.5,
                                                                            in1=xt[:, s, n * 512:(n + 1) * 512], op0=ALU.mult, op1=ALU.add),
                     reads=[('scr', jt), ('xt', s)], writes=[('xt', s)])
            ro = t * 512 + s * 128
            S.dma('sp', lambda e, ro=ro, s=s: e.dma_start(out=out_d[ro:ro + 128, :], in_=xt[:, s, :]), semkey=('st', s), reads=[('xt', s)])
        slot_release(base + 33)
        slot_release(base + 34)

    S.emit([('st', s) for s in range(4)])
    return nc


def _rope_consts():
    inv = (np.float32(10000.0) ** (-np.arange(0, 64, 2, dtype=np.float32) / np.float32(64))).astype(np.float32)
    pos = np.arange(2048, dtype=np.float32)
    ang = (pos[:, None] * inv[None, :]).astype(np.float32)
    c = np.cos(ang).astype(np.float32)
    s = np.sin(ang).astype(np.float32)
    cos2 = np.concatenate([c, c], axis=1)
    sins = np.concatenate([-s, s], axis=1)
    tab = np.stack([cos2, sins], axis=1)
    tab = tab.reshape(16, 128, 2, 64).transpose(1, 0, 2, 3)
    return np.ascontiguousarray(tab, dtype=np.float32)


def _const_inputs():
    k = np.arange(128)[:, None]
    q = np.arange(128)[None, :]
    m_cur = (q >= k).astype(np.float32)
    m_prev = (k > q).astype(np.float32)
    mask = np.stack([m_cur, m_prev], axis=1).astype(ml_dtypes.bfloat16)
    ident = np.eye(128, dtype=np.float32).astype(ml_dtypes.bfloat16)
    return {"rope": _rope_consts(), "mask": np.ascontiguousarray(mask), "ident": ident}


def prep_shared(inputs):
    f = lambda a: np.ascontiguousarray(np.asarray(a, dtype=np.float32))
    L0 = lambda name: f(inputs[name])[0]
    vecs = [L0('g_mix'), L0('g_mlp'), L0('g_ple'), L0('conv_w')[0], L0('conv_w')[1], L0('conv_w')[2], L0('conv_w')[3],
            L0('conv_b'), L0('b_rg'), L0('b_ig'), L0('lru_lambda')]
    fmv = np.stack(vecs, axis=0).reshape(11, 8, 128).transpose(2, 0, 1)
    gq = L0('q_gain')
    gk = L0('k_gain')
    bc = np.concatenate([gq, np.roll(gq, 32), gk, np.roll(gk, 32), L0('sinks')])
    bcv = np.broadcast_to(bc[None, :], (128, 272))
    d = {
        "w_in": L0('w_in'), "w_rnn_proj": L0('w_rnn_proj'), "w_attn_proj": L0('w_attn_proj'), "w_out": L0('w_out'),
        "w_up": L0('w_up'), "w_down": L0('w_down'), "w_ple_gate": L0('w_ple_gate'), "w_ple_proj": L0('w_ple_proj'),
        "w_rg": L0('w_rg'), "w_ig": L0('w_ig'),
        "fmv": np.ascontiguousarray(fmv, dtype=np.float32), "bcv": np.ascontiguousarray(bcv, dtype=np.float32),
    }
    d.update(_const_inputs())
    return d


def kernel(**inputs):
    x = np.asarray(inputs['x'], dtype=np.float32)
    p = np.asarray(inputs['p'], dtype=np.float32)[0]
    B, Sq, D = x.shape
    per = B // N_CORES
    shared = prep_shared(inputs)
    nc = build_nc(per, Sq // 512)
    in_maps = []
    for c in range(N_CORES):
        m = dict(shared)
        m["x"] = np.ascontiguousarray(x[c * per:(c + 1) * per].reshape(per * Sq, D))
        m["p"] = np.ascontiguousarray(p[c * per:(c + 1) * per].reshape(per * Sq, 256))
        in_maps.append(m)
    res = run_bass_kernel_spmd(nc, in_maps, core_ids=list(range(N_CORES)))
    outs = [np.asarray(r["out"], dtype=np.float32).reshape(per, Sq, D) for r in res.results]
    return np.concatenate(outs, axis=0)
```

```python
import numpy as np
import ml_dtypes
import concourse.bass as bass
import concourse.mybir as mybir
from concourse.bass_utils import run_bass_kernel_spmd
from concourse.alu_op_type import AluOpType as ALU

AF = mybir.ActivationFunctionType
AX = mybir.AxisListType
F32 = mybir.dt.float32
BF16 = mybir.dt.bfloat16

N_CORES = 8
IN_COLS = [0, 2048, 2560, 3072, 1024, 1536, 3584, 4096, 512, 4608, 5120]
NSTREAM = 35


class Sched:
    ENG = ('pe', 'act', 'dve', 'pool', 'sp')

    def __init__(self, nc):
        self.nc = nc
        self.q = {e: [] for e in self.ENG}
        self.sem = {}
        self.cnt = {}
        self.waited = {e: {} for e in self.ENG}
        self.lw = {}
        self.rd = {}
        self.nsem = 0
        for e in ('pe', 'act', 'dve', 'pool'):
            self._mksem(e)

    def _mksem(self, k):
        if k not in self.sem:
            self.sem[k] = self.nc.alloc_semaphore(name="sm%d" % self.nsem)
            self.nsem += 1
            self.cnt[k] = 0

    def _deps(self, e, reads, writes):
        d = {}

        def need(k, v):
            if d.get(k, 0) < v:
                d[k] = v
        for r in reads:
            w = self.lw.get(r)
            if w:
                need(*w)
            if r[0] == 'ps':
                for k, v in self.rd.get(r, {}).items():
                    if k != e:
                        need(k, v)
        for w_ in writes:
            w = self.lw.get(w_)
            if w:
                need(*w)
            for k, v in self.rd.get(w_, {}).items():
                need(k, v)
        waits = []
        for k, v in d.items():
            if self.waited[e].get(k, 0) < v:
                self.waited[e][k] = v
                waits.append((k, v))
        return waits

    def _record(self, semkey, val, reads, writes):
        for r in reads:
            m = self.rd.setdefault(r, {})
            if m.get(semkey, 0) < val:
                m[semkey] = val
        for w_ in writes:
            self.lw[w_] = (semkey, val)
            self.rd[w_] = {}

    def op(self, e, fn, reads=(), writes=()):
        waits = self._deps(e, reads, writes)
        self.cnt[e] += 1
        self.q[e].append((fn, waits, e, 1))
        self._record(e, self.cnt[e], reads, writes)

    def dma(self, e, fn, semkey, reads=(), writes=(), record=True):
        self._mksem(semkey)
        waits = self._deps(e, reads, writes)
        self.cnt[semkey] += 16
        self.q[e].append((fn, waits, semkey, 16))
        if record:
            self._record(semkey, self.cnt[semkey], reads, writes)

    def emit(self, final_keys):
        nc = self.nc
        with nc.Block() as block:
            def mk(e):
                def body(eng):
                    for fn, waits, sk, n in self.q[e]:
                        for k, v in waits:
                            eng.wait_ge(self.sem[k], v)
                        ins = fn(eng)
                        ins.then_inc(self.sem[sk], n)
                    if e == 'sp':
                        for k in final_keys:
                            eng.wait_ge(self.sem[k], self.cnt[k])
                return body
            block.tensor(mk('pe'))
            block.scalar(mk('act'))
            block.vector(mk('dve'))
            block.gpsimd(mk('pool'))
            block.sync(mk('sp'))


def build_nc(n_seq, tps, NSLOT=4, NSCR=8, CONV_AHEAD=6):
    NT = n_seq * tps
    ntok = NT * 512
    nc = bass.Bass("TRN2", target_bir_lowering=False)
    S = Sched(nc)

    def din(name, shape, dt=F32):
        return nc.dram_tensor(name, shape, dt, kind="ExternalInput").ap()
    x_d = din("x", [ntok, 1024])
    p_d = din("p", [ntok, 256])
    w_in_d = din("w_in", [1024, 5632])
    w_rnn_d = din("w_rnn_proj", [1024, 1024])
    w_attn_d = din("w_attn_proj", [1024, 1024])
    w_out_d = din("w_out", [1024, 1024])
    w_up_d = din("w_up", [1024, 4096])
    w_down_d = din("w_down", [4096, 1024])
    w_pg_d = din("w_ple_gate", [1024, 1024])
    w_pp_d = din("w_ple_proj", [256, 1024])
    w_rg_d = din("w_rg", [16, 64, 64])
    w_ig_d = din("w_ig", [16, 64, 64])
    fmv_d = din("fmv", [128, 11, 8])
    bcv_d = din("bcv", [128, 272])
    rope_d = din("rope", [128, 16, 2, 64])
    ident_d = din("ident", [128, 128], BF16)
    mask_d = din("mask", [128, 2, 128], BF16)
    out_d = nc.dram_tensor("out", [ntok, 1024], F32, kind="ExternalOutput").ap()

    s_all = nc.dram_tensor("s_all", [NSTREAM, 1024, 512], BF16, kind="Internal").ap()

    def sb(name, shape, dt):
        return nc.alloc_sbuf_tensor("sb_" + name, shape, dt).ap()
    xt = sb("xt", [128, 4, 1024], F32)
    htm = sb("htm", [128, 4, 1024], BF16)
    mg = htm.rearrange("p s (a b) -> p (s a) b", b=512)
    hT = sb("hT", [128, 8, 512], BF16)
    big = sb("big", [128, 32, 512], BF16)
    qtm = big[:, 24:32, :].rearrange("p (s a) b -> p s (a b)", a=2)
    xcbf = sb("xcbf", [128, 4, 512], BF16)
    yain = sb("yain", [128, 8, 512], BF16)
    e_sb = yain.rearrange("p (s a) b -> p s (a b)", a=2)
    ktm = sb("ktm", [128, 4, 256], BF16)
    QT = sb("QT", [128, 2, 4, 512], BF16)
    KT = sb("KT", [128, 4, 4, 128], BF16)
    VA = sb("VA", [128, 8, 4, 65], BF16)
    xr = sb("xr", [128, 4, 516], BF16)
    dg = sb("dg", [128, 8, 4, 128], BF16)
    aa = sb("aa", [128, 4, 512], F32)
    a2 = sb("a2", [128, 4, 512], F32)
    hist = sb("hist", [128, 8, 4], BF16)
    state = sb("state", [128, 8], F32)
    scr = sb("scr", [128, NSCR, 512], F32)
    NPTB = 8
    NSQB = 4
    sqb = sb("sqb", [128, NSQB, 512], BF16)
    ptb = sb("ptb", [128, NPTB, 512], BF16)
    pbf = sb("pbf", [128, 4, 256], BF16)
    pT = sb("pT", [128, 2, 512], BF16)
    tabraw = sb("tabraw", [128, 4, 2, 64], F32)
    tabt = sb("tabt", [128, 4, 4, 64], F32)
    wsl = sb("wsl", [128, NSLOT, 8, 512], BF16)
    bd = sb("bd", [128, 2, 8, 128], BF16)
    wple = sb("wple", [128, 2, 1024], BF16)
    fmv = sb("fmv", [128, 11, 8], F32)
    bcv = sb("bcv", [128, 272], F32)
    ident = sb("ident", [128, 128], BF16)
    mask = sb("mask", [128, 2, 128], BF16)
    cn = sb("cn", [128, 16], F32)
    hb = sb("hb", [128, 2, 8], F32)
    es = sb("es", [128, 16], F32)
    nh = sb("nh", [128, 8], F32)
    tmp8 = sb("tmp8", [128, 8], F32)
    ss = sb("ss", [128, 4], F32)
    rstd = sb("rstd", [128, 4], F32)
    ssq = sb("ssq", [128, 4, 20], F32)
    rq = sb("rq", [128, 4, 20], F32)
    NDEN = 4
    den = sb("den", [128, NDEN, 4], F32)
    ps = nc.alloc_psum_tensor("ps", [128, 8, 512], F32).ap()

    st = {'bank': 0, 'scr': 0, 'ptb': 0, 'den': 0, 'next_load': 0, 'sqb': 0}

    def PS(b):
        return ps[:, b, :]

    def PSB(b):
        return ps[:, b, :].bitcast(BF16)

    def SC(j):
        return scr[:, j, :]

    def SCb(j):
        return scr[:, j, :].bitcast(BF16)


    def bank():
        b = st['bank']
        st['bank'] = (b + 1) % 8
        return b

    def scr_next():
        j = st['scr']
        st['scr'] = (j + 1) % NSCR
        return j

    def ptb_next():
        j = st['ptb']
        st['ptb'] = (j + 1) % NPTB
        return j

    def den_next():
        j = st['den']
        st['den'] = (j + 1) % NDEN
        return j

    def k_tm(name, s):
        return [(name, 2 * s), (name, 2 * s + 1)]

    def k_all(name, n=8):
        return [(name, i) for i in range(n)]

    def conv_src(i):
        if i < 11:
            return w_in_d[:, IN_COLS[i]:IN_COLS[i] + 512]
        if i < 13:
            return w_attn_d[:, (i - 11) * 512:(i - 10) * 512]
        if i < 15:
            return w_rnn_d[:, (i - 13) * 512:(i - 12) * 512]
        if i < 17:
            return w_out_d[:, (i - 15) * 512:(i - 14) * 512]
        if i < 25:
            return w_up_d[:, (i - 17) * 512:(i - 16) * 512]
        if i < 33:
            n, g = divmod(i - 25, 4)
            return w_down_d[g * 1024:(g + 1) * 1024, n * 512:(n + 1) * 512]
        return w_pg_d[:, (i - 33) * 512:(i - 32) * 512]

    st['next_conv'] = 0

    def convert_upto(i):
        while st['next_conv'] <= min(i, NSTREAM - 1):
            k = st['next_conv']
            src = conv_src(k)
            S.dma('pool', lambda e, k=k, src=src: e.dma_start(out=s_all[k], in_=src), semkey=('cv', k), writes=[('wscr', k)])
            st['next_conv'] += 1

    for dst, src, key in [(fmv, fmv_d, 'c_fmv'), (bcv, bcv_d, 'c_bcv'), (ident, ident_d, 'c_ident'), (mask, mask_d, 'c_mask')]:
        S.dma('sp', lambda e, dst=dst, src=src: e.dma_start(out=dst[:], in_=src[:]), semkey=('cst', key), writes=[(key,)])
    convert_upto(3)
    S.op('pool', lambda e: e.memset(nh[:], -0.5), writes=[('nh',)])

    def late_setup():
        S.op('pool', lambda e: e.memset(bd[:], 0.0), writes=[('bd',)])
        for gi, wsrc in enumerate([w_rg_d, w_ig_d]):
            v = wsrc.rearrange("(c two) i j -> two i c j", two=2)
            S.dma('pool', lambda e, gi=gi, v=v: e.dma_start(out=bd[0:64, gi, :, 0:64], in_=v[0]), semkey=('bdl',),
                  writes=[('bd',)], record=False)
            S.dma('pool', lambda e, gi=gi, v=v: e.dma_start(out=bd[64:128, gi, :, 64:128], in_=v[1]), semkey=('bdl',),
                  writes=[('bd',)], record=False)
        S.lw[('bd',)] = (('bdl',), S.cnt[('bdl',)])
        S.dma('pool', lambda e: e.dma_start(out=wple[:], in_=w_pp_d.rearrange("(kc p) n -> p kc n", p=128)), semkey=('wpl',),
              writes=[('wple',)])
        S.op('pool', lambda e: e.memset(VA[:], 1.0), writes=k_all('VA'))
        S.op('pool', lambda e: e.memset(QT[:], 0.0), writes=[('QT', a_, g_) for a_ in range(2) for g_ in range(4)])
        S.op('pool', lambda e: e.memset(KT[:], 0.0), writes=[('KT', i_) for i_ in range(4)])
        S.op('act', lambda e: e.activation(out=tmp8[:], in_=fmv[:, 10, :], func=AF.Exp, scale=-1.0), reads=[('c_fmv',)], writes=[('tmp8',)])
        S.op('act', lambda e: e.activation(out=tmp8[:], in_=tmp8[:], func=AF.Ln, bias=1.0), reads=[('tmp8',)], writes=[('tmp8',)])
        S.op('dve', lambda e: e.tensor_scalar(out=cn[:, 0:8], in0=tmp8[:], scalar1=-8.0, scalar2=None, op0=ALU.mult), reads=[('tmp8',)], writes=[('cn',)])
        S.op('dve', lambda e: e.tensor_scalar(out=cn[:, 8:16], in0=tmp8[:], scalar1=-4.0, scalar2=None, op0=ALU.mult), reads=[('tmp8',)], writes=[('cn',)])
        S.op('dve', lambda e: e.tensor_scalar(out=hb[:], in0=fmv[:, 8:10, :], scalar1=0.5, scalar2=None, op0=ALU.mult), reads=[('c_fmv',)], writes=[('hb',)])
        S.op('act', lambda e: e.activation(out=es[:], in_=bcv[:, 256:272], func=AF.Exp), reads=[('c_bcv',)], writes=[('es',)])

        for c in range(8):
            for tap in range(4):
                S.op('dve', lambda e, c=c, tap=tap: e.tensor_scalar(out=dg[:, c, tap, :], in0=ident[:], scalar1=fmv[:, 3 + tap, c:c + 1], scalar2=None, op0=ALU.mult),
                     reads=[('c_ident',), ('c_fmv',)], writes=[('dg',)])


    total_loads = NT * NSTREAM

    def slot_src(i):
        return s_all[i].rearrange("(kc p) n -> p kc n", p=128), i

    def prefetch(upto):
        while st['next_load'] <= min(upto, total_loads - 1):
            k = st['next_load']
            pos = k % NSLOT
            convert_upto(k + CONV_AHEAD)
            src, mname = slot_src(k % NSTREAM)
            S.dma('sp', lambda e, pos=pos, src=src: e.dma_start(out=wsl[:, pos], in_=src), semkey=('wl', pos),
                  reads=[('wscr', mname)], writes=[('wsl', pos)])
            st['next_load'] += 1

    st['released'] = -1

    def slot_acquire(k):
        assert k <= st['released'] + NSLOT, (k, st['released'])
        prefetch(k)
        return k % NSLOT

    def slot_release(k):
        assert k == st['released'] + 1, (k, st['released'])
        st['released'] = k
        prefetch(k + NSLOT)

    def pe_fine(mk, pos, b, rkeys):
        for kc in range(8):
            S.op('pe', mk(kc), reads=[('wsl', pos), (rkeys, kc)], writes=([('ps', b)] if kc in (0, 7) else []))

    def mm_fm(pos, oc, rhs3, b):
        def f(e):
            for kc in range(8):
                ins = e.matmul(PS(b), lhsT=wsl[:, pos, kc, oc * 128:(oc + 1) * 128], rhs=rhs3[:, kc, :],
                               start=(kc == 0), stop=(kc == 7))
            return ins
        return f

    def mm_tm(pos, s, lhs3, b):
        def f(e):
            for kc in range(8):
                ins = e.matmul(PS(b), lhsT=lhs3[:, kc, s * 128:(s + 1) * 128], rhs=wsl[:, pos, kc, :],
                               start=(kc == 0), stop=(kc == 7))
            return ins
        return f

    xstage = big[:, 0:16, :].bitcast(F32).rearrange("p (s a) b -> p s (a b)", a=4)

    def k_xs(s):
        return [('big', 4 * s + i) for i in range(4)]

    def x_load(t, s):
        r0 = t * 512 + s * 128
        S.dma('sp', lambda e, r0=r0, s=s: e.dma_start(out=xstage[:, s, :], in_=x_d[r0:r0 + 128, :]), semkey=('xl', s),
              writes=k_xs(s))

    def x_commit():
        for s in range(4):
            S.dma('sp', lambda e, s=s: e.dma_start(out=xt[:, s, :], in_=xstage[:, s, :]), semkey=('xc', s), reads=k_xs(s), writes=[('xt', s)])

    def norm_a(gi, staged=False):
        for s in range(4):
            j_ = scr_next()
            src = xstage[:, s, :] if staged else xt[:, s, :]
            skeys = k_xs(s) if staged else [('xt', s)]
            S.op('act', lambda e, s=s, j_=j_, src=src: e.activation(out=SCb(j_), in_=src, func=AF.Square, accum_out=ss[:, s:s + 1]),
                 reads=skeys, writes=[('scr', j_), ('ss', s)])
            S.op('pool', lambda e, s=s: e.tensor_scalar(out=rstd[:, s:s + 1], in0=ss[:, s:s + 1], scalar1=1.0 / 1024, scalar2=1e-6,
                                                        op0=ALU.mult, op1=ALU.add), reads=[('ss', s)], writes=[('rstd', s)])
            S.op('pool', lambda e, s=s: e.tensor_tensor(out=rstd[:, s:s + 1], in0=rstd[:, s:s + 1], in1=nh[:, 0:1], op=ALU.pow),
                 reads=[('rstd', s), ('nh',)], writes=[('rstd', s)])
            S.op('dve', lambda e, s=s, src=src: e.tensor_scalar(out=htm[:, s, :], in0=src, scalar1=rstd[:, s:s + 1], scalar2=None, op0=ALU.mult),
                 reads=skeys + [('rstd', s)], writes=k_tm('htm', s))

    def norm_stage(gi):
        norm_a(gi)
        norm_b(gi)

    def norm_b(gi):
        for c in range(8):
            b = bank()

            def f(e, c=c, b=b):
                for s in range(4):
                    ins = e.transpose(out=PSB(b)[:, s * 128:(s + 1) * 128], in_=htm[:, s, c * 128:(c + 1) * 128], identity=ident[:])
                return ins
            S.op('pe', f, reads=k_all('htm') + [('c_ident',)], writes=[('ps', b)])
            if c % 2 == 0:
                S.op('act', lambda e, c=c, b=b: e.activation(out=hT[:, c, :], in_=PSB(b)[:, 0:512], func=AF.Copy, scale=fmv[:, gi, c:c + 1]),
                     reads=[('ps', b), ('c_fmv',)], writes=[('hT', c)])
            else:
                S.op('dve', lambda e, c=c, b=b: e.tensor_scalar(out=hT[:, c, :], in0=PSB(b)[:, 0:512], scalar1=fmv[:, gi, c:c + 1], scalar2=None, op0=ALU.mult),
                     reads=[('ps', b), ('c_fmv',)], writes=[('hT', c)])

    def rope_chain(s, b, col0, nh_, ti, dst3, hoff, dkeys):
        w = nh_ * 64
        isq = st['sqb']
        st['sqb'] = (isq + 1) % NSQB
        S.op('act', lambda e: e.activation(out=sqb[:, isq, 0:w], in_=PS(b)[:, col0:col0 + w], func=AF.Square), reads=[('ps', b)], writes=[('sqb', isq)])
        S.op('dve', lambda e: e.tensor_reduce(out=ssq[:, s, hoff:hoff + nh_], in_=sqb[:, isq, 0:w].rearrange("p (h d) -> p h d", d=64), axis=AX.X, op=ALU.add),
             reads=[('sqb', isq)], writes=[('ssq', s, hoff)])
        S.op('pool', lambda e: e.tensor_scalar(out=rq[:, s, hoff:hoff + nh_], in0=ssq[:, s, hoff:hoff + nh_], scalar1=1.0 / 64, scalar2=1e-6, op0=ALU.mult, op1=ALU.add),
             reads=[('ssq', s, hoff)], writes=[('rq', s, hoff)])
        S.op('pool', lambda e: e.tensor_tensor(out=rq[:, s, hoff:hoff + nh_], in0=rq[:, s, hoff:hoff + nh_], in1=nh[:, 0:nh_], op=ALU.pow),
             reads=[('rq', s, hoff), ('nh',)], writes=[('rq', s, hoff)])
        j1 = scr_next()
        j2 = scr_next()
        q3 = PS(b)[:, col0:col0 + w].rearrange("p (h d) -> p h d", d=64)
        t1 = SC(j1)[:, 0:w].rearrange("p (h d) -> p h d", d=64)
        t2 = SC(j2)[:, 0:w].rearrange("p (h d) -> p h d", d=64)
        cc = tabt[:, s, ti, :].unsqueeze(1).broadcast_to([128, nh_, 64])
        wlo = tabt[:, s, ti + 1, 0:32].unsqueeze(1).broadcast_to([128, nh_, 32])
        whi = tabt[:, s, ti + 1, 32:64].unsqueeze(1).broadcast_to([128, nh_, 32])
        S.op('dve', lambda e: e.tensor_tensor(out=t1, in0=q3, in1=cc, op=ALU.mult), reads=[('ps', b), ('tabt',)], writes=[('scr', j1)])
        S.op('dve', lambda e: e.tensor_tensor(out=t2[:, :, 0:32], in0=q3[:, :, 32:64], in1=wlo, op=ALU.mult), reads=[('ps', b), ('tabt',)], writes=[('scr', j2)])
        S.op('dve', lambda e: e.tensor_tensor(out=t2[:, :, 32:64], in0=q3[:, :, 0:32], in1=whi, op=ALU.mult), reads=[('ps', b), ('tabt',)], writes=[('scr', j2)])
        S.op('pool', lambda e: e.tensor_tensor(out=SC(j1)[:, 0:w], in0=SC(j1)[:, 0:w], in1=SC(j2)[:, 0:w], op=ALU.add), reads=[('scr', j1), ('scr', j2)], writes=[('scr', j1)])
        rb = rq[:, s, hoff:hoff + nh_].unsqueeze(2).broadcast_to([128, nh_, 64])

        def tail():
            S.op('dve', lambda e: e.tensor_tensor(out=dst3, in0=t1, in1=rb, op=ALU.mult), reads=[('scr', j1), ('rq', s, hoff)], writes=dkeys)
        rope_flush()
        rope_pending.append(tail)

    rope_pending = []

    def rope_flush():
        while rope_pending:
            rope_pending.pop(0)()

    gel = big[:, 0:8, :]
    ta = big[:, 8:16, :]
    tb = big[:, 16:24, :]

    def rnn_chunk_front(c):
        cc_ = c % 4
        br = bank()
        S.op('pe', lambda e: e.matmul(PS(br), lhsT=bd[:, 0, c, :], rhs=xcbf[:, cc_, :], start=True, stop=True), reads=[('bd',), ('xcbf', cc_)], writes=[('ps', br)])
        bi = bank()
        S.op('pe', lambda e: e.matmul(PS(bi), lhsT=bd[:, 1, c, :], rhs=xcbf[:, cc_, :], start=True, stop=True), reads=[('bd',), ('xcbf', cc_)], writes=[('ps', bi)])
        jr = scr_next()
        S.op('act', lambda e: e.activation(out=SC(jr), in_=PS(br), func=AF.Tanh, scale=0.5, bias=hb[:, 0, c:c + 1]), reads=[('ps', br), ('hb',)], writes=[('scr', jr)])
        S.op('act', lambda e: e.activation(out=aa[:, cc_, :], in_=SC(jr), func=AF.Exp, scale=cn[:, 8 + c:9 + c], bias=cn[:, 8 + c:9 + c]), reads=[('scr', jr), ('cn',)], writes=[('aa', cc_)])
        S.op('act', lambda e: e.activation(out=a2[:, cc_, :], in_=SC(jr), func=AF.Exp, scale=cn[:, c:c + 1], bias=cn[:, c:c + 1]), reads=[('scr', jr), ('cn',)], writes=[('a2', cc_)])
        ji = scr_next()
        S.op('act', lambda e: e.activation(out=SC(ji), in_=PS(bi), func=AF.Tanh, scale=0.5, bias=hb[:, 1, c:c + 1]), reads=[('ps', bi), ('hb',)], writes=[('scr', ji)])
        S.op('dve', lambda e: e.scalar_tensor_tensor(out=xcbf[:, cc_, :], in0=SC(ji), scalar=1.0, in1=xcbf[:, cc_, :], op0=ALU.add, op1=ALU.mult),
             reads=[('scr', ji), ('xcbf', cc_)], writes=[('xcbf', cc_)])
        S.op('dve', lambda e: e.tensor_scalar(out=a2[:, cc_, :], in0=a2[:, cc_, :], scalar1=1.0, scalar2=None, op0=ALU.min), reads=[('a2', cc_)], writes=[('a2', cc_)])

    def rnn_sqrt(c):
        cc_ = c % 4
        S.op('act', lambda e: e.activation(out=a2[:, cc_, :], in_=a2[:, cc_, :], func=AF.Sqrt, scale=-0.25, bias=0.25), reads=[('a2', cc_)], writes=[('a2', cc_)])

    def rnn_chunk_back(c):
        cc_ = c % 4
        S.op('pool', lambda e: e.tensor_tensor(out=a2[:, cc_, :], in0=a2[:, cc_, :], in1=xcbf[:, cc_, :], op=ALU.mult),
             reads=[('a2', cc_), ('xcbf', cc_)], writes=[('a2', cc_)])
        jh = scr_next()
        S.op('dve', lambda e: e.tensor_tensor_scan(out=SC(jh), data0=aa[:, cc_, :], data1=a2[:, cc_, :], initial=state[:, c:c + 1], op0=ALU.mult, op1=ALU.add),
             reads=[('aa', cc_), ('a2', cc_), ('state', c)], writes=[('scr', jh)])
        S.op('dve', lambda e: e.tensor_copy(out=state[:, c:c + 1], in_=SC(jh)[:, 511:512]), reads=[('scr', jh)], writes=[('state', c)])
        S.op('pool', lambda e: e.tensor_tensor(out=yain[:, c, :], in0=SC(jh), in1=gel[:, c, :], op=ALU.mult), reads=[('scr', jh), ('big', c)], writes=[('yain', c)])

    def rope_tables(jj):
        S.dma('sp', lambda e, jj=jj: e.dma_start(out=tabraw[:], in_=rope_d[:, jj * 4:(jj + 1) * 4]), semkey=('tl',), writes=[('tabraw',)])
        for ti, g0 in [(0, 0), (1, 64), (2, 128), (3, 192)]:
            S.op('dve', lambda e, ti=ti, g0=g0: e.tensor_tensor(out=tabt[:, :, ti, :], in0=tabraw[:, :, ti % 2, :],
                                                               in1=bcv[:, g0:g0 + 64].unsqueeze(1).broadcast_to([128, 4, 64]), op=ALU.mult),
                 reads=[('tabraw',), ('c_bcv',)], writes=[('tabt',)])

    for s in range(4):
        x_load(0, s)
    rope_tables(0)
    norm_a(0, staged=True)
    late_setup()
    prefetch(NSLOT - 1)

    for t in range(NT):
        j = t % tps
        base = t * NSTREAM
        if j == 0:
            S.op('pool', lambda e: e.memset(hist[:], 0.0), writes=k_all('hist'))
            S.op('pool', lambda e: e.memset(state[:], 0.0), writes=k_all('state'))
        r0 = t * 512
        S.dma('pool', lambda e, r0=r0: e.dma_start(out=pbf[:], in_=p_d[r0:r0 + 512, :].rearrange("(s p) k -> p s k", p=128)), semkey=('pl',),
              writes=[('pbf',)])

        x_commit()
        norm_b(0)

        def q_block(n):
            pos = slot_acquire(base + 1 + n)
            for s in range(4):
                b = bank()
                S.op('pe', mm_tm(pos, s, hT, b), reads=[('wsl', pos)] + k_all('hT'), writes=[('ps', b)])
                dst3 = qtm[:, s, n * 512:(n + 1) * 512].rearrange("p (h d) -> p h d", d=64)
                rope_chain(s, b, 0, 8, 0, dst3, n * 8, [('big', 24 + 2 * s + n)])
            slot_release(base + 1 + n)

        def kv_block():
            pos = slot_acquire(base + 3)
            for s in range(4):
                blk = j * 4 + s
                b = bank()
                S.op('pe', mm_tm(pos, s, hT, b), reads=[('wsl', pos)] + k_all('hT'), writes=[('ps', b)])
                S.op('dve', lambda e, b=b, blk=blk: e.tensor_copy(out=VA[:, blk % 8, :, 0:64], in_=PS(b)[:, 256:512].rearrange("p (h d) -> p h d", d=64)),
                     reads=[('ps', b)], writes=[('VA', blk % 8)])
                dst3 = ktm[:, s, :].rearrange("p (h d) -> p h d", d=64)
                rope_chain(s, b, 0, 4, 2, dst3, 16, [('ktm', s)])
            slot_release(base + 3)

        def xrnn_a(n, fine=False):
            kk = base + (0 if n == 0 else 8)
            pos = slot_acquire(kk)
            for oc in range(4):
                c = n * 4 + oc
                b = bank()
                xb = c % 4
                if fine and oc == 0:
                    pe_fine(lambda kc, pos=pos, b=b: (lambda e: e.matmul(PS(b), lhsT=wsl[:, pos, kc, 0:128], rhs=hT[:, kc, :], start=(kc == 0), stop=(kc == 7))), pos, b, 'hT')
                else:
                    S.op('pe', mm_fm(pos, oc, hT, b), reads=[('wsl', pos)] + k_all('hT'), writes=[('ps', b)])
                S.op('act', lambda e, b=b, xb=xb: e.activation(out=xr[:, xb, 3:515], in_=PS(b), func=AF.Copy), reads=[('ps', b)], writes=[('xr', xb)])
                S.op('dve', lambda e, c=c, xb=xb: e.tensor_copy(out=xr[:, xb, 0:3], in_=hist[:, c, 0:3]), reads=[('hist', c)], writes=[('xr', xb)])
            slot_release(kk)

        def xrnn_b(n):
            for oc in range(4):
                c = n * 4 + oc
                xb = c % 4
                bc = bank()

                def fc(e, c=c, xb=xb, bc=bc):
                    for tap in range(4):
                        ins = e.matmul(PS(bc), lhsT=dg[:, c, tap, :], rhs=xr[:, xb, tap:tap + 512], start=(tap == 0), stop=(tap == 3))
                    return ins
                S.op('pe', fc, reads=[('xr', xb), ('dg',)], writes=[('ps', bc)])
                S.op('act', lambda e, c=c, bc=bc: e.activation(out=xcbf[:, c % 4, :], in_=PS(bc), func=AF.Identity, bias=fmv[:, 7, c:c + 1]),
                     reads=[('ps', bc), ('c_fmv',)], writes=[('xcbf', c % 4)])
                S.op('dve', lambda e, c=c, xb=xb: e.tensor_copy(out=hist[:, c, 0:3], in_=xr[:, xb, 512:515]), reads=[('xr', xb)], writes=[('hist', c)])

        def gelu_block(n):
            pos = slot_acquire(base + 4 + n)
            for oc in range(4):
                c = n * 4 + oc
                b = bank()
                S.op('pe', mm_fm(pos, oc, hT, b), reads=[('wsl', pos)] + k_all('hT'), writes=[('ps', b)])
                S.op('act', lambda e, b=b, c=c: e.activation(out=gel[:, c, :], in_=PS(b), func=AF.Gelu_apprx_tanh), reads=[('ps', b)], writes=[('big', c)])
            slot_release(base + 4 + n)

        def gate_block(n):
            kk = base + (6 + n if n < 2 else 7 + n)
            pos = slot_acquire(kk)
            for oc in range(4):
                cg = n * 4 + oc
                b = bank()
                S.op('pe', mm_fm(pos, oc, hT, b), reads=[('wsl', pos)] + k_all('hT'), writes=[('ps', b)])
                S.op('act', lambda e, b=b, cg=cg: e.activation(out=big[:, 8 + cg, :], in_=PS(b), func=AF.Tanh, scale=0.5), reads=[('ps', b)], writes=[('big', 8 + cg)])
            slot_release(kk)

        def attn_T(s):
            blk = j * 4 + s
            b = bank()

            def fk(e, b=b, s=s):
                for g in range(4):
                    ins = e.transpose(out=PSB(b)[0:64, g * 128:(g + 1) * 128], in_=ktm[:, s, g * 64:(g + 1) * 64], identity=ident[:])
                return ins
            S.op('pe', fk, reads=[('ktm', s), ('c_ident',)], writes=[('ps', b)])
            S.op('dve', lambda e, b=b, blk=blk: e.tensor_copy(out=KT[0:64, blk % 4, :, :].rearrange("p g k -> p (g k)"), in_=PSB(b)[0:64, 0:512]),
                 reads=[('ps', b)], writes=[('KT', blk % 4)])
            for g in range(4):
                b = bank()

                def fq(e, b=b, s=s, g=g):
                    for i in range(4):
                        h = 4 * g + i
                        ins = e.transpose(out=PSB(b)[0:64, i * 128:(i + 1) * 128], in_=qtm[:, s, h * 64:(h + 1) * 64], identity=ident[:])
                    return ins
                S.op('pe', fq, reads=[('big', 24 + 2 * s + (g // 2)), ('c_ident',)], writes=[('ps', b)])
                if g % 2 == 0:
                    S.op('act', lambda e, b=b, s=s, g=g: e.activation(out=QT[0:64, s % 2, g, :], in_=PSB(b)[0:64, 0:512], func=AF.Copy), reads=[('ps', b)], writes=[('QT', s % 2, g)])
                else:
                    S.op('dve', lambda e, b=b, s=s, g=g: e.tensor_copy(out=QT[0:64, s % 2, g, :], in_=PSB(b)[0:64, 0:512]), reads=[('ps', b)], writes=[('QT', s % 2, g)])

        pend = []

        def flush_pv():
            bo_g, pts, g, s = pend.pop(0)
            bo = bank()

            def fpv(e, bo=bo, pts=pts, g=g):
                for i in range(4):
                    for n_, (ip, kblk) in enumerate(pts):
                        ins = e.matmul(PS(bo)[:, i * 65:(i + 1) * 65], lhsT=ptb[:, ip, i * 128:(i + 1) * 128], rhs=VA[:, kblk % 8, g, :],
                                       start=(n_ == 0), stop=(n_ == len(pts) - 1))
                return ins
            S.op('pe', fpv, reads=[('ptb', ip) for ip, _ in pts] + [('VA', kblk % 8) for _, kblk in pts], writes=[('ps', bo)])
            dn = den_next()
            o3 = PS(bo)[:, 0:260].rearrange("p (i d) -> p i d", d=65)
            S.op('dve', lambda e, o3=o3, dn=dn, g=g: e.tensor_tensor(out=den[:, dn, :], in0=o3[:, :, 64], in1=es[:, 4 * g:4 * g + 4], op=ALU.add),
                 reads=[('ps', bo), ('es',)], writes=[('den', dn)])
            S.op('dve', lambda e, dn=dn: e.reciprocal(out=den[:, dn, :], in_=den[:, dn, :]), reads=[('den', dn)], writes=[('den', dn)])
            S.op('dve', lambda e, o3=o3, dn=dn, g=g, s=s: e.tensor_tensor(out=htm[:, s, g * 256:(g + 1) * 256].rearrange("p (i d) -> p i d", d=64),
                                                                         in0=o3[:, :, 0:64], in1=den[:, dn, :].unsqueeze(2).broadcast_to([128, 4, 64]), op=ALU.mult),
                 reads=[('ps', bo), ('den', dn)], writes=k_tm('htm', s))

        def attn_gen():
          for s in range(4):
            blk = j * 4 + s
            kbs = ([(blk - 1, 1)] if blk > 0 else []) + [(blk, 0)]
            for g in range(4):
                pts = []
                for kblk, m in kbs:
                    b = bank()
                    S.op('pe', lambda e, b=b, kblk=kblk, g=g, s=s: e.matmul(PS(b), lhsT=KT[:, kblk % 4, g, :], rhs=QT[:, s % 2, g, :], start=True, stop=True),
                         reads=[('KT', kblk % 4), ('QT', s % 2, g)], writes=[('ps', b)])
                    ip = ptb_next()
                    S.op('act', lambda e, b=b, ip=ip: e.activation(out=ptb[:, ip, :], in_=PS(b), func=AF.Exp, scale=0.125), reads=[('ps', b)], writes=[('ptb', ip)])
                    S.op('pool', lambda e, ip=ip, m=m: e.tensor_tensor(out=ptb[:, ip, :].rearrange("p (i q) -> p i q", q=128),
                                                                     in0=ptb[:, ip, :].rearrange("p (i q) -> p i q", q=128),
                                                                     in1=mask[:, m, :].unsqueeze(1).broadcast_to([128, 4, 128]), op=ALU.mult),
                         reads=[('ptb', ip), ('c_mask',)], writes=[('ptb', ip)])
                    pts.append((ip, kblk))
                pend.append((None, pts, g, s))
                if len(pend) > 2:
                    flush_pv()
                if g == 1 and s < 3:
                    attn_T(s + 1)
                yield

        ag = attn_gen()

        def A(n):
            for _ in range(n):
                next(ag, None)

        xrnn_a(0, fine=True)
        q_block(0)
        xrnn_b(0)
        q_block(1)
        kv_block()
        rope_flush()
        if t + 1 < NT:
            rope_tables((t + 1) % tps)
        gelu_block(0)
        gelu_block(1)
        attn_T(0)
        A(2)
        rnn_chunk_front(0)
        rnn_chunk_front(1)
        A(2)
        gate_block(0)
        rnn_chunk_front(2)
        rnn_chunk_front(3)
        A(2)
        gate_block(1)
        for c in range(0, 4):
            rnn_sqrt(c)
        A(2)
        xrnn_a(1)
        rnn_chunk_back(0)
        rnn_chunk_back(1)
        A(2)
        gate_block(2)
        rnn_chunk_back(2)
        rnn_chunk_back(3)
        A(2)
        xrnn_b(1)
        rnn_chunk_front(4)
        rnn_chunk_front(5)
        A(2)
        rnn_chunk_front(6)
        rnn_chunk_front(7)
        gate_block(3)
        A(2)
        A(16)
        while pend:
            flush_pv()
        for c in range(4, 8):
            rnn_sqrt(c)
        for c in range(8):
            b = bank()

            def fo(e, c=c, b=b):
                for s in range(4):
                    ins = e.transpose(out=PSB(b)[:, s * 128:(s + 1) * 128], in_=htm[:, s, c * 128:(c + 1) * 128], identity=ident[:])
                return ins
            S.op('pe', fo, reads=k_all('htm') + [('c_ident',)], writes=[('ps', b)])
            if c % 2 == 0:
                S.op('act', lambda e, c=c, b=b: e.activation(out=hT[:, c, :], in_=PSB(b)[:, 0:512], func=AF.Copy), reads=[('ps', b)], writes=[('hT', c)])
            else:
                S.op('dve', lambda e, c=c, b=b: e.tensor_copy(out=hT[:, c, :], in_=PSB(b)[:, 0:512]), reads=[('ps', b)], writes=[('hT', c)])

        for c in range(4, 8):
            rnn_chunk_back(c)

        for n in range(2):
            pos = slot_acquire(base + 11 + n)
            for oc in range(4):
                c = n * 4 + oc
                b = bank()
                S.op('pe', mm_fm(pos, oc, hT, b), reads=[('wsl', pos)] + k_all('hT'), writes=[('ps', b)])
                S.op('dve', lambda e, b=b, c=c: e.scalar_tensor_tensor(out=mg[:, c, :], in0=tb[:, c, :], scalar=1.0, in1=PS(b), op0=ALU.add, op1=ALU.mult),
                     reads=[('ps', b), ('big', 16 + c)], writes=[('htm', c)])
            slot_release(base + 11 + n)
        for n in range(2):
            pos = slot_acquire(base + 13 + n)
            for oc in range(4):
                c = n * 4 + oc
                b = bank()
                S.op('pe', mm_fm(pos, oc, yain, b), reads=[('wsl', pos)] + k_all('yain'), writes=[('ps', b)])
                jt = scr_next()
                S.op('dve', lambda e, b=b, c=c, jt=jt: e.scalar_tensor_tensor(out=SCb(jt)[:, 0:512], in0=ta[:, c, :], scalar=1.0, in1=PS(b), op0=ALU.add, op1=ALU.mult),
                     reads=[('ps', b), ('big', 8 + c)], writes=[('scr', jt)])
                S.op('pool', lambda e, c=c, jt=jt: e.tensor_tensor(out=mg[:, c, :], in0=mg[:, c, :], in1=SCb(jt)[:, 0:512], op=ALU.add),
                     reads=[('scr', jt), ('htm', c)], writes=[('htm', c)])
            slot_release(base + 13 + n)

        pos0 = slot_acquire(base + 15)
        pos1 = slot_acquire(base + 16)
        for s in range(4):
            for n, pos in enumerate((pos0, pos1)):
                b = bank()
                S.op('pe', mm_tm(pos, s, mg, b), reads=[('wsl', pos)] + k_all('htm'), writes=[('ps', b)])
                S.op('dve', lambda e, b=b, s=s, n=n: e.scalar_tensor_tensor(out=xt[:, s, n * 512:(n + 1) * 512], in0=PS(b), scalar=0.5,
                                                                          in1=xt[:, s, n * 512:(n + 1) * 512], op0=ALU.mult, op1=ALU.add),
                     reads=[('ps', b), ('xt', s)], writes=[('xt', s)])
        slot_release(base + 15)
        slot_release(base + 16)

        norm_a(1)
        for kc in range(2):
            b = bank()

            def fp(e, kc=kc, b=b):
                for s in range(4):
                    ins = e.transpose(out=PSB(b)[:, s * 128:(s + 1) * 128], in_=pbf[:, s, kc * 128:(kc + 1) * 128], identity=ident[:])
                return ins
            S.op('pe', fp, reads=[('pbf',), ('c_ident',)], writes=[('ps', b)])
            S.op('dve', lambda e, kc=kc, b=b: e.tensor_copy(out=pT[:, kc, :], in_=PSB(b)[:, 0:512]), reads=[('ps', b)], writes=[('pT', kc)])
        for s in range(4):
            for n in range(2):
                be = bank()

                def fe(e, be=be, s=s, n=n):
                    for kc in range(2):
                        ins = e.matmul(PS(be), lhsT=pT[:, kc, s * 128:(s + 1) * 128], rhs=wple[:, kc, n * 512:(n + 1) * 512], start=(kc == 0), stop=(kc == 1))
                    return ins
                S.op('pe', fe, reads=[('pT', 0), ('pT', 1), ('wple',)], writes=[('ps', be)])
                S.op('act', lambda e, be=be, s=s, n=n: e.activation(out=e_sb[:, s, n * 512:(n + 1) * 512], in_=PS(be), func=AF.Copy),
                     reads=[('ps', be)], writes=[('yain', 2 * s + n)])
        norm_b(1)
        for n in range(8):
            pos = slot_acquire(base + 17 + n)
            for oc in range(4):
                u = n * 4 + oc
                b = bank()
                if u == 0:
                    pe_fine(lambda kc, pos=pos, b=b: (lambda e: e.matmul(PS(b), lhsT=wsl[:, pos, kc, 0:128], rhs=hT[:, kc, :], start=(kc == 0), stop=(kc == 7))), pos, b, 'hT')
                else:
                    S.op('pe', mm_fm(pos, oc, hT, b), reads=[('wsl', pos)] + k_all('hT'), writes=[('ps', b)])
                jt = scr_next()
                S.op('act', lambda e, b=b, jt=jt: e.activation(out=SCb(jt)[:, 0:512], in_=PS(b), func=AF.Relu), reads=[('ps', b)], writes=[('scr', jt)])
                eng = 'dve' if u % 2 == 0 else 'pool'
                S.op(eng, lambda e, u=u, jt=jt: e.tensor_tensor(out=big[:, u, :], in0=SCb(jt)[:, 0:512], in1=SCb(jt)[:, 0:512], op=ALU.mult),
                     reads=[('scr', jt)], writes=[('big', u)])
            slot_release(base + 17 + n)
        for n in range(2):
            B = [bank() for _ in range(4)]
            for g in range(4):
                pos = slot_acquire(base + 25 + n * 4 + g)
                for s in range(4):
                    def fd(e, pos=pos, g=g, s=s, bb=B[s]):
                        for kc in range(8):
                            ins = e.matmul(PS(bb), lhsT=big[:, g * 8 + kc, s * 128:(s + 1) * 128], rhs=wsl[:, pos, kc, :],
                                           start=(g == 0 and kc == 0), stop=(g == 3 and kc == 7))
                        return ins
                    wr = [('ps', B[s])] if g in (0, 3) else []
                    S.op('pe', fd, reads=[('wsl', pos)] + [('big', g * 8 + kc) for kc in range(8)], writes=wr)
                slot_release(base + 25 + n * 4 + g)
            for s in range(4):
                S.op('dve', lambda e, s=s, n=n, bb=B[s]: e.tensor_tensor(out=xt[:, s, n * 512:(n + 1) * 512], in0=PS(bb), in1=xt[:, s, n * 512:(n + 1) * 512], op=ALU.add),
                     reads=[('ps', B[s]), ('xt', s)], writes=[('xt', s)])

        if t + 1 < NT:
            for s in range(4):
                x_load(t + 1, s)
        norm_stage(2)
        if t + 1 < NT:
            norm_a(0, staged=True)
        pos0 = slot_acquire(base + 33)
        pos1 = slot_acquire(base + 34)
        for s in range(4):
            for n, pos in enumerate((pos0, pos1)):
                bg = bank()
                if s == 0 and n == 0:
                    pe_fine(lambda kc, pos=pos, bg=bg: (lambda e: e.matmul(PS(bg), lhsT=hT[:, kc, 0:128], rhs=wsl[:, pos, kc, :], start=(kc == 0), stop=(kc == 7))), pos, bg, 'hT')
                else:
                    S.op('pe', mm_tm(pos, s, hT, bg), reads=[('wsl', pos)] + k_all('hT'), writes=[('ps', bg)])
                jt = scr_next()
                S.op('act', lambda e, bg=bg, jt=jt: e.activation(out=SC(jt), in_=PS(bg), func=AF.Tanh, scale=0.5), reads=[('ps', bg)], writes=[('scr', jt)])
                S.op('dve', lambda e, jt=jt, s=s, n=n: e.scalar_tensor_tensor(out=SC(jt), in0=SC(jt), scalar=1.0, in1=e_sb[:, s, n * 512:(n + 1) * 512], op0=ALU.add, op1=ALU.mult),
                     reads=[('scr', jt), ('yain', 2 * s + n)], writes=[('scr', jt)])
                S.op('dve', lambda e, jt=jt, s=s, n=n: e.scalar_tensor_tensor(out=xt[:, s, n * 512:(n + 1) * 512], in0=SC(jt), scalar=0.5,
                                                                            in1=xt[:, s, n * 512:(n + 1) * 512], op0=ALU.mult, op1=ALU.add),
                     reads=[('scr', jt), ('xt', s)], writes=[('xt', s)])
            ro = t * 512 + s * 128
            S.dma('sp', lambda e, ro=ro, s=s: e.dma_start(out=out_d[ro:ro + 128, :], in_=xt[:, s, :]), semkey=('st', s), reads=[('xt', s)])
        slot_release(base + 33)
        slot_release(base + 34)

    S.emit([('st', s) for s in range(4)])
    return nc


def _rope_consts():
    inv = (np.float32(10000.0) ** (-np.arange(0, 64, 2, dtype=np.float32) / np.float32(64))).astype(np.float32)
    pos = np.arange(2048, dtype=np.float32)
    ang = (pos[:, None] * inv[None, :]).astype(np.float32)
    c = np.cos(ang).astype(np.float32)
    s = np.sin(ang).astype(np.float32)
    cos2 = np.concatenate([c, c], axis=1)
    sins = np.concatenate([-s, s], axis=1)
    tab = np.stack([cos2, sins], axis=1)
    tab = tab.reshape(16, 128, 2, 64).transpose(1, 0, 2, 3)
    return np.ascontiguousarray(tab, dtype=np.float32)


def _const_inputs():
    k = np.arange(128)[:, None]
    q = np.arange(128)[None, :]
    m_cur = (q >= k).astype(np.float32)
    m_prev = (k > q).astype(np.float32)
    mask = np.stack([m_cur, m_prev], axis=1).astype(ml_dtypes.bfloat16)
    ident = np.eye(128, dtype=np.float32).astype(ml_dtypes.bfloat16)
    return {"rope": _rope_consts(), "mask": np.ascontiguousarray(mask), "ident": ident}


def prep_shared(inputs):
    f = lambda a: np.ascontiguousarray(np.asarray(a, dtype=np.float32))
    L0 = lambda name: f(inputs[name])[0]
    vecs = [L0('g_mix'), L0('g_mlp'), L0('g_ple'), L0('conv_w')[0], L0('conv_w')[1], L0('conv_w')[2], L0('conv_w')[3],
            L0('conv_b'), L0('b_rg'), L0('b_ig'), L0('lru_lambda')]
    fmv = np.stack(vecs, axis=0).reshape(11, 8, 128).transpose(2, 0, 1)
    gq = L0('q_gain')
    gk = L0('k_gain')
    bc = np.concatenate([gq, np.roll(gq, 32), gk, np.roll(gk, 32), L0('sinks')])
    bcv = np.broadcast_to(bc[None, :], (128, 272))
    d = {
        "w_in": L0('w_in'), "w_rnn_proj": L0('w_rnn_proj'), "w_attn_proj": L0('w_attn_proj'), "w_out": L0('w_out'),
        "w_up": L0('w_up'), "w_down": L0('w_down'), "w_ple_gate": L0('w_ple_gate'), "w_ple_proj": L0('w_ple_proj'),
        "w_rg": L0('w_rg'), "w_ig": L0('w_ig'),
        "fmv": np.ascontiguousarray(fmv, dtype=np.float32), "bcv": np.ascontiguousarray(bcv, dtype=np.float32),
    }
    d.update(_const_inputs())
    return d


def kernel(**inputs):
    x = np.asarray(inputs['x'], dtype=np.float32)
    p = np.asarray(inputs['p'], dtype=np.float32)[0]
    B, Sq, D = x.shape
    per = B // N_CORES
    shared = prep_shared(inputs)
    nc = build_nc(per, Sq // 512)
    in_maps = []
    for c in range(N_CORES):
        m = dict(shared)
        m["x"] = np.ascontiguousarray(x[c * per:(c + 1) * per].reshape(per * Sq, D))
        m["p"] = np.ascontiguousarray(p[c * per:(c + 1) * per].reshape(per * Sq, 256))
        in_maps.append(m)
    res = run_bass_kernel_spmd(nc, in_maps, core_ids=list(range(N_CORES)))
    outs = [np.asarray(r["out"], dtype=np.float32).reshape(per, Sq, D) for r in res.results]
    return np.concatenate(outs, axis=0)
```
